# Optimizing a Trainium2 kernel written in Bass

```python
import jax, jax.numpy as jnp
from jax import lax
import numpy as np

D_MODEL = 1024
BATCH = 4
SEQ = 4096
DEPTH = 4

HEAD_DIM = 64
FOX_HEADS = 8
SB_HEADS = 4
POOL_GROUPS = 4
POOL_WINDOWS = (2, 4, 8, 16)
POOL_GROUP_DIM = 64
FOX_W = FOX_HEADS * HEAD_DIM
SB_W = SB_HEADS * HEAD_DIM
POOL_W = POOL_GROUPS * POOL_GROUP_DIM
D_MIX = FOX_W + POOL_W + SB_W
Q_BLOCK = 128
EPS = 1e-6
NEG = -1e30

IN_SPLITS = (
    FOX_W, FOX_W, FOX_W, FOX_W,
    FOX_HEADS,
    POOL_W, POOL_W,
    SB_W, SB_W, SB_W, SB_W,
)
D_IN = sum(IN_SPLITS)

kernel_name = "hybrid_fox_pool_stickbreak_parallel_heads"


def rms_norm(x, g):
    xf = x.astype(jnp.float32)
    y = xf * lax.rsqrt(jnp.mean(xf * xf, axis=-1, keepdims=True) + EPS)
    return (y * g.astype(jnp.float32)).astype(x.dtype)


def to_heads(t, n_heads):
    b, s, _ = t.shape
    return t.reshape(b, s, n_heads, HEAD_DIM).transpose(0, 2, 1, 3)


def from_heads(t):
    b, h, s, d = t.shape
    return t.transpose(0, 2, 1, 3).reshape(b, s, h * d)


def split_blocks(t):
    b, h, s = t.shape[:3]
    nb = s // Q_BLOCK
    t = t.reshape((b, h, nb, Q_BLOCK) + t.shape[3:])
    return jnp.moveaxis(t, 2, 0)


def merge_blocks(o):
    nb, b, h, qb, d = o.shape
    return jnp.moveaxis(o, 0, 2).reshape(b, h, nb * qb, d)


def forgetting_attention(q, k, v, log_f):
    s_len, d = q.shape[2], q.shape[3]
    c = jnp.cumsum(log_f, axis=-1)
    kpos = jnp.arange(s_len)
    scale = d ** -0.5

    def one_block(args):
        qi, ci, i = args
        qpos = i * Q_BLOCK + jnp.arange(Q_BLOCK)
        sc = jnp.einsum('bhqd,bhkd->bhqk', qi, k).astype(jnp.float32) * scale
        sc = sc + (ci[..., :, None] - c[..., None, :])
        sc = jnp.where(kpos[None, :] <= qpos[:, None], sc, NEG)
        p = jax.nn.softmax(sc, axis=-1)
        return jnp.einsum('bhqk,bhkd->bhqd', p.astype(v.dtype), v)

    nb = s_len // Q_BLOCK
    o = lax.map(one_block, (split_blocks(q), split_blocks(c), jnp.arange(nb)))
    return merge_blocks(o)


def stick_breaking_attention(q, k, v):
    s_len, d = q.shape[2], q.shape[3]
    kpos = jnp.arange(s_len)
    scale = d ** -0.5

    def one_block(args):
        qi, i = args
        qpos = i * Q_BLOCK + jnp.arange(Q_BLOCK)
        z = jnp.einsum('bhqd,bhkd->bhqk', qi, k).astype(jnp.float32) * scale
        causal = kpos[None, :] < qpos[:, None]
        log_1m_beta = jnp.where(causal, -jax.nn.softplus(z), 0.0)
        rest = lax.cumsum(log_1m_beta, axis=3, reverse=True) - log_1m_beta
        a = jnp.where(causal, jnp.exp(jax.nn.log_sigmoid(z) + rest), 0.0)
        return jnp.einsum('bhqk,bhkd->bhqd', a.astype(v.dtype), v)

    nb = s_len // Q_BLOCK
    o = lax.map(one_block, (split_blocks(q), jnp.arange(nb)))
    return merge_blocks(o)


def causal_window_mean(x, w):
    s_len = x.shape[1]
    xf = x.astype(jnp.float32)
    cs = jnp.cumsum(xf, axis=1)
    cs_prev = jnp.pad(cs, ((0, 0), (w, 0), (0, 0)))[:, :s_len]
    count = jnp.minimum(jnp.arange(s_len) + 1, w).astype(jnp.float32)
    return ((cs - cs_prev) / count[None, :, None]).astype(x.dtype)


def multiscale_pool(x, w_pool, scale):
    b, s_len, _ = x.shape
    groups = jnp.split(x, POOL_GROUPS, axis=-1)
    pooled = jnp.stack([causal_window_mean(g, w) - g for g, w in zip(groups, POOL_WINDOWS)], axis=2)
    y = jnp.einsum('bsgc,gcd->bsgd', pooled, w_pool).reshape(b, s_len, POOL_W)
    return y * scale


def hybrid_layer(x, norm_g, w_in, b_f, q_norm_g, k_norm_g, w_pool, pool_scale, w_out):
    h = rms_norm(x, norm_g)
    proj = jnp.einsum('bsd,de->bse', h, w_in)
    idx = np.cumsum(IN_SPLITS)[:-1].tolist()
    (fq, fk, fv, fg, ff, px, pg, sq, sk, sv, sg) = jnp.split(proj, idx, axis=-1)

    fq = rms_norm(to_heads(fq, FOX_HEADS), q_norm_g)
    fk = rms_norm(to_heads(fk, FOX_HEADS), k_norm_g)
    fv = to_heads(fv, FOX_HEADS)
    log_f = jax.nn.log_sigmoid((ff + b_f).astype(jnp.float32)).transpose(0, 2, 1)
    fox_out = from_heads(forgetting_attention(fq, fk, fv, log_f)) * jax.nn.silu(fg)

    pool_out = multiscale_pool(px, w_pool, pool_scale) * jax.nn.silu(pg)

    sb = stick_breaking_attention(to_heads(sq, SB_HEADS), to_heads(sk, SB_HEADS), to_heads(sv, SB_HEADS))
    sb_out = from_heads(sb) * jax.nn.silu(sg)

    mixed = jnp.concatenate([fox_out, pool_out, sb_out], axis=-1)
    return x + jnp.einsum('bse,ed->bsd', mixed, w_out)


def setup_inputs(seed: int = 0) -> dict:
    key = jax.random.key(seed)
    ks = jax.random.split(key, 10)
    x = jax.random.normal(ks[0], (BATCH, SEQ, D_MODEL), jnp.float32)
    norm_g = 1.0 + 0.02 * jax.random.normal(ks[1], (DEPTH, D_MODEL), jnp.float32)
    w_in = jax.random.normal(ks[2], (DEPTH, D_MODEL, D_IN), jnp.float32) * D_MODEL ** -0.5
    b_f = jax.random.uniform(ks[3], (DEPTH, FOX_HEADS), jnp.float32, 1.0, 4.0)
    q_norm_g = 1.0 + 0.02 * jax.random.normal(ks[4], (DEPTH, HEAD_DIM), jnp.float32)
    k_norm_g = 1.0 + 0.02 * jax.random.normal(ks[5], (DEPTH, HEAD_DIM), jnp.float32)
    w_pool = jax.random.normal(ks[6], (DEPTH, POOL_GROUPS, POOL_GROUP_DIM, POOL_GROUP_DIM), jnp.float32) * POOL_GROUP_DIM ** -0.5
    pool_scale = 1.0 + 0.02 * jax.random.normal(ks[7], (DEPTH, POOL_W), jnp.float32)
    w_out = jax.random.normal(ks[8], (DEPTH, D_MIX, D_MODEL), jnp.float32) * D_MIX ** -0.5
    return {"x": x, "norm_g": norm_g, "w_in": w_in, "b_f": b_f, "q_norm_g": q_norm_g,
            "k_norm_g": k_norm_g, "w_pool": w_pool, "pool_scale": pool_scale, "w_out": w_out}


def reference(x, norm_g, w_in, b_f, q_norm_g, k_norm_g, w_pool, pool_scale, w_out):
    for l in range(DEPTH):
        x = hybrid_layer(x, norm_g[l], w_in[l], b_f[l], q_norm_g[l], k_norm_g[l],
                         w_pool[l], pool_scale[l], w_out[l])
    return x
```

```python
import contextlib
import numpy as np
import ml_dtypes
import concourse.bass as bass
import concourse.mybir as mybir
from concourse.bass_utils import run_bass_kernel_spmd

F32 = mybir.dt.float32
BF16 = mybir.dt.bfloat16
AF = mybir.ActivationFunctionType
ALU = mybir.AluOpType
AX = mybir.AxisListType

S = 4096
D = 1024
NT = 32
NBLK = 8
DIN = 3592
DEPTH = 4
EPS = 1e-6
NEGBIG = -30000.0

C_FQ, C_FK, C_FV, C_FG, C_FF, C_PX, C_PG, C_SQ, C_SK, C_SV, C_SG = (
    0, 512, 1024, 1536, 2048, 2056, 2312, 2568, 2824, 3080, 3336)

CB_ID = 0
CB_UNEG = 128
CB_ONEG = 256
CB_MBF = 384
CB_MBS = CB_MBF + 4 * 512
CB_POOL = CB_MBS + 4 * 512
CB_L = CB_POOL + 12 * 128
CB_E = CB_L + 128
CB_N = CB_E + 128
CF_L = 0
CF_E = 128
CF_ID = 256
CF_N = 384

DEBUG = {}


class Buf:
    __slots__ = ("name", "w", "r")

    def __init__(self, name):
        self.name = name
        self.w = None
        self.r = []


class Sched:
    def __init__(self, nc, stack):
        self.nc = nc
        self.sems = {}
        self.E = {}
        for name, h in (("pe", nc.tensor), ("act", nc.scalar), ("dve", nc.vector),
                        ("pool", nc.gpsimd), ("sp", nc.sync)):
            self.sems["s_" + name] = stack.enter_context(nc.semaphore("s_" + name))
            self.E[name] = dict(h=h, cnt=0, ops=[], known={}, dn=0, dcnt=[])
        for name, k in (("sp", 28), ("pool", 12), ("act", 8)):
            for i in range(k):
                self.sems[f"d_{name}{i}"] = stack.enter_context(nc.semaphore(f"d_{name}{i}"))
                self.E[name]["dcnt"].append(0)

    def _emit(self, en, fn, reads, writes, inc=True, dsem=None, dval=None, extra=()):
        e = self.E[en]
        need = {}

        def add(tok, raw):
            if tok is None:
                return
            sname, val, src = tok
            if src == en and en == "pe":
                return
            if need.get(sname, 0) < val:
                need[sname] = val

        for b in reads:
            add(b.w, True)
        for b in writes:
            add(b.w, False)
            for t in b.r:
                add(t, False)
        for sname, val in extra:
            if need.get(sname, 0) < val:
                need[sname] = val
        waits = []
        for sname, val in need.items():
            if e["known"].get(sname, 0) < val:
                e["known"][sname] = val
                waits.append((sname, val))
        if dsem is None:
            tok = ("s_" + en, e["cnt"] + 1, en)
            if inc:
                e["cnt"] += 1
        else:
            tok = (dsem, dval, "dma:" + en)
        e["ops"].append((waits, fn, inc, dsem))
        for b in writes:
            b.w = tok
            b.r = []
        for b in reads:
            b.r = [t for t in b.r if t[2] != tok[2] or tok[2].startswith("dma")] + [tok]
        return tok

    def op(self, en, fn, reads=(), writes=(), inc=True):
        return self._emit(en, fn, list(reads), list(writes), inc=inc)

    def dma(self, q, out, in_, reads=(), writes=()):
        e = self.E[q]
        i = e["dn"] % len(e["dcnt"])
        e["dn"] += 1
        sname = f"d_{q}{i}"
        prev = 16 * e["dcnt"][i]
        e["dcnt"][i] += 1
        val = 16 * e["dcnt"][i]
        fn = lambda h, out=out, in_=in_: h.dma_start(out=out, in_=in_)
        return self._emit(q, fn, list(reads), list(writes), inc=False, dsem=sname, dval=val,
                          extra=[(sname, prev)] if prev else [])

    def finish(self, en, toks):
        extra = [(t[0], t[1]) for t in toks]
        self._emit(en, None, [], [], inc=False, extra=extra)
        for q, e in self.E.items():
            ex = [(f"d_{q}{i}", 16 * c) for i, c in enumerate(e["dcnt"]) if c]
            if ex:
                self._emit(q, None, [], [], inc=False, extra=ex)

    def replay(self, block):
        sems = self.sems

        def run(en, h):
            own = sems["s_" + en]
            for waits, fn, inc, dsem in self.E[en]["ops"]:
                for sname, val in waits:
                    h.wait_ge(sems[sname], val)
                if fn is None:
                    continue
                ins = fn(h)
                if dsem is not None:
                    ins.then_inc(sems[dsem], 16)
                elif inc:
                    ins.then_inc(own, 1)

        @block.tensor
        def _(h):
            run("pe", h)

        @block.scalar
        def _(h):
            run("act", h)

        @block.vector
        def _(h):
            run("dve", h)

        @block.gpsimd
        def _(h):
            run("pool", h)

        @block.sync
        def _(h):
            run("sp", h)


def _host_consts():
    p = np.arange(128)[:, None]
    f = np.arange(512)[None, :]
    f128 = np.arange(128)[None, :]
    cb = np.zeros((128, CB_N), np.float32)
    cb[:, CB_ID:CB_ID + 128] = np.eye(128)
    cb[:, CB_UNEG:CB_UNEG + 128] = -1.0 * (p >= f128)
    cb[:, CB_ONEG:CB_ONEG + 128] = -1.0
    for o in range(4):
        cb[:, CB_MBF + o * 512:CB_MBF + (o + 1) * 512] = np.where(f >= p + 128 * o, 0.0, NEGBIG)
        cb[:, CB_MBS + o * 512:CB_MBS + (o + 1) * 512] = np.where(f > p + 128 * o, 0.0, NEGBIG)
    for g, w in enumerate((2, 4, 8, 16)):
        s_ = p
        t_ = f128
        same = np.where((s_ <= t_) & (s_ > t_ - w), 1.0 / w, 0.0) - 1.0 * (s_ == t_)
        prev = np.where(s_ - 128 > t_ - w, 1.0 / w, 0.0)
        cnt = np.minimum(t_ + 1, w)
        first = np.where((s_ <= t_) & (s_ > t_ - w), 1.0 / cnt, 0.0) - 1.0 * (s_ == t_)
        base = CB_POOL + (g * 3) * 128
        cb[:, base:base + 128] = same
        cb[:, base + 128:base + 256] = prev
        cb[:, base + 256:base + 384] = first
    cb[:, CB_L:CB_L + 128] = 1.0 * (p <= f128)
    cb[:, CB_E:CB_E + 128] = 1.0 * ((p == 127) & (f128 >= 0))
    cf = np.zeros((128, CF_N), np.float32)
    cf[:, CF_L:CF_L + 128] = 1.0 * (p <= f128)
    cf[:, CF_E:CF_E + 128] = 1.0 * ((p == 127) & (f128 >= 0))
    cf[:, CF_ID:CF_ID + 128] = np.eye(128)
    return cb.astype(ml_dtypes.bfloat16), cf


def build_program(depth=DEPTH, dump=False):
    nc = bass.Bass("TRN2", target_bir_lowering=False)
    dt = nc.dram_tensor
    bigkind = "Internal" if DEBUG.get("fake_inputs") else "ExternalInput"
    x_in = dt("x", [S, D], F32, kind=bigkind).ap()
    w_in = dt("w_in", [DEPTH, D, DIN], F32, kind=bigkind).ap()
    w_out = dt("w_out", [DEPTH, D, D], F32, kind=bigkind).ap()
    w_pool = dt("w_pool", [DEPTH, 4, 64, 64], F32, kind="ExternalInput").ap()
    gcol_d = dt("gcol", [128, DEPTH * 8], F32, kind="ExternalInput").ap()
    gq_d = dt("gq_bc", [128, DEPTH * 64], F32, kind="ExternalInput").ap()
    gk_d = dt("gk_bc", [128, DEPTH * 64], F32, kind="ExternalInput").ap()
    bf_d = dt("bf_bc", [128, DEPTH * 8], F32, kind="ExternalInput").ap()
    ps_d = dt("ps_bc", [128, DEPTH * 256], F32, kind="ExternalInput").ap()
    cb_d = dt("cb", [128, CB_N], BF16, kind="ExternalInput").ap()
    cf_d = dt("cf", [128, CF_N], F32, kind="ExternalInput").ap()
    out_d = dt("out", [S, D], F32, kind="ExternalOutput").ap()

    xs_d = [dt("xs0", [S, D], F32).ap(), dt("xs1", [S, D], F32).ap()]
    wbi_d = dt("wbi", [DEPTH, 128, 8, DIN], BF16).ap()
    wbo_d = dt("wbo", [DEPTH, 128, 8, D], BF16).ap()
    qt_d = dt("qt", [8, 67, S], BF16).ap()
    kt_d = dt("kt", [8, 67, S], BF16).ap()
    vf_d = dt("vf", [128, 8, NT, 65], BF16).ap()
    qs_d = dt("qs", [4, 64, S], BF16).ap()
    ks_d = dt("ks", [4, 64, S], BF16).ap()
    vs_d = dt("vs", [128, 4, NT, 64], BF16).ap()
    dbg = {}
    if dump:
        dbg["qt"] = dt("dbg_qt", [8, 67, S], BF16, kind="ExternalOutput").ap()
        dbg["kt"] = dt("dbg_kt", [8, 67, S], BF16, kind="ExternalOutput").ap()
        dbg["gm"] = dt("dbg_gm", [S, D], BF16, kind="ExternalOutput").ap()
        dbg["c"] = dt("dbg_c", [128, NT * 8], F32, kind="ExternalOutput").ap()

    with contextlib.ExitStack() as st:
        sb = lambda name, shape, dtype: st.enter_context(nc.sbuf_tensor("sb_" + name, shape, dtype))
        sc = Sched(nc, st)

        GM = sb("GM", [128, NT, D], BF16)
        cb = sb("cb", [128, CB_N], BF16)
        cf = sb("cf", [128, CF_N], F32)
        gcol = sb("gcol", [128, DEPTH * 8], F32)
        gq = sb("gq", [128, DEPTH * 64], F32)
        gk = sb("gk", [128, DEPTH * 64], F32)
        gqk = sb("gqk", [128, 64], F32)
        gqk8 = sb("gqk8", [128, 512], F32)
        bfb = sb("bfb", [128, DEPTH * 8], F32)
        psb = sb("psb", [128, 256], F32)
        wp_st_full = sb("wp_st", [128, 4, 64], F32)
        wp_st = wp_st_full[0:64]
        wp_bf_full = sb("wp_bf", [128, 4, 64], BF16)
        wp_bf = wp_bf_full[0:64]
        c_sb = sb("c_sb", [128, NT, 8], F32)
        negc = sb("negc", [128, NT, 8], F32)
        wstage = [sb(f"wstage{i}", [128, 898], F32) for i in range(2)]
        wcv = [sb(f"wcv{i}", [128, 898], BF16) for i in range(2)]
        big = [sb(f"big{i}", [128, 4096], BF16) for i in range(2)]
        xt = [sb(f"xt{i}", [128, D], F32) for i in range(2)]
        hn = [sb(f"hn{i}", [128, D], BF16) for i in range(2)]
        hT = sb("hT", [128, 8, 512], BF16)
        wff = sb("wff", [128, 8, 8], BF16)
        ss = sb("ss", [128, 4], F32)
        epsb = sb("epsb", [128, 1], F32)
        rstd = sb("rstd", [128, 4], F32)
        junk = sb("junk", [128, D], BF16)
        sqtmp = sb("sqtmp", [128, 512], F32)
        ktmp = sb("ktmp", [128, 512], F32)
        ssq = sb("ssq", [128, 16], F32)
        rsq = sb("rsq", [128, 16], F32)
        qa = sb("qa", [128, 8, 68], BF16)
        ka = sb("ka", [128, 8, 64], BF16)
        sqk = sb("sqk", [128, 512], BF16)
        lf_t = sb("lf_t", [128, 6, 8], F32)
        lf = sb("lf", [128, 8], F32)
        csp = sb("csp", [128, 6, 8], F32)
        csb = sb("csb", [128, NT, 3, 8], BF16)
        lfs = sb("lfs", [128, 3, 8], BF16)
        QTst = sb("QTst", [128, 8, 256], BF16)
        KTst = sb("KTst", [128, 8, 256], BF16)
        QsTst = sb("QsTst", [128, 2, 256], BF16)
        KsTst = sb("KsTst", [128, 2, 256], BF16)
        Vst = sb("Vst", [128, 8, 4, 65], BF16)
        sVst = sb("sVst", [128, 4, 4, 64], BF16)
        pxt = sb("pxt", [128, 5, 256], BF16)
        pooledT_full = sb("pooledT", [128, 4, 128], BF16)
        pooledT = pooledT_full[0:64]
        ytmp = sb("ytmp", [128, 256], BF16)
        o2 = sb("o2", [128, 260], F32)
        o3 = sb("o3", [128, 256], BF16)
        Vh = [sb(f"Vh{i}", [128, NT, 65], BF16) for i in range(2)]
        QTb = [sb(f"QTb{i}", [128, 512], BF16) for i in range(2)]
        Pt = [sb(f"Pt{i}", [128, 512], BF16) for i in range(3)]
        rec = sb("rec", [128, 4], F32)
        osb = [sb(f"osb{i}", [128, 512], F32) for i in range(2)]
        e_sb = [sb(f"e_sb{i}", [128, 512], F32) for i in range(2)]
        sp_sb = [sb(f"sp_sb{i}", [128, 512], BF16) for i in range(2)]
        arg_sb = [sb(f"arg_sb{i}", [128, 512], F32) for i in range(2)]
        carry = sb("carry", [128, 512], F32)
        xo = [sb(f"xo{i}", [128, D], F32) for i in range(2)]
        PS = [st.enter_context(nc.psum_tensor(f"ps{i}", [128, 512], F32)) for i in range(8)]

        B = {}

        def bf(name):
            if name not in B:
                B[name] = Buf(name)
            return B[name]

        vfB = {j: [bf(f"vfx{j}")] for j in range(NBLK)}
        vsB = {j: [bf(f"vsx{j}")] for j in range(NBLK)}
        ident = cb[:, CB_ID:CB_ID + 128]
        Lmat = cb[:, CB_L:CB_L + 128]
        Esel = cb[:, CB_E:CB_E + 128]
        identf = cf[:, CF_ID:CF_ID + 128]

        sc.dma("sp", cb[:, :], cb_d[:, :], writes=[bf("cb")])
        sc.dma("sp", cf[:, :], cf_d[:, :], writes=[bf("cf")])
        sc.dma("sp", gcol[:, :], gcol_d[:, :], writes=[bf("gcol")])
        sc.dma("sp", gq[:, :], gq_d[:, :], writes=[bf("gq")])
        sc.dma("sp", gk[:, :], gk_d[:, :], writes=[bf("gk")])
        sc.dma("sp", bfb[:, :], bf_d[:, :], writes=[bf("bfb")])
        sc.op("pool", lambda h: h.memset(Vst[:, :, :, 64:65], 1.0), writes=[bf("Vst")])
        sc.op("pool", lambda h: h.memset(KTst[64:67, :, :], 1.0), writes=[bf("KTst")])
        sc.op("pool", lambda h: h.memset(qa[:, :, 67:68], 0.0), writes=[bf("qa")])
        sc.op("pool", lambda h: h.memset(epsb[:, :], EPS), writes=[bf("epsb")])

        wq = ["sp", "pool"]

        wbi_bufs = {l: [] for l in range(DEPTH)}
        wbo_bufs = {l: [] for l in range(DEPTH)}

        def convert_weights(l):
            k = 0
            for kt in range(8):
                for pc in range(4):
                    c0 = pc * 898
                    stg, sB = wstage[k % 2], bf(f"wstage{k % 2}")
                    cv, cB = wcv[k % 2], bf(f"wcv{k % 2}")
                    sc.dma("sp", stg[:, 0:898], w_in[l, kt * 128:(kt + 1) * 128, c0:c0 + 898],
                           writes=[sB])
                    sc.op("pool", lambda h, stg=stg, cv=cv, kt=kt, l=l: h.tensor_scalar(
                        out=cv[:, 0:898], in0=stg[:, 0:898],
                        scalar1=gcol[:, l * 8 + kt:l * 8 + kt + 1], scalar2=None, op0=ALU.mult),
                        reads=[sB, bf("gcol")], writes=[cB])
                    wb = Buf(f"wbi{l}_{k}")
                    wbi_bufs[l].append(wb)
                    sc.dma("pool", wbi_d[l, :, kt, c0:c0 + 898], cv[:, 0:898], reads=[cB], writes=[wb])
                    k += 1
            for kt in range(8):
                for pc in range(2):
                    c0 = pc * 512
                    stg, sB = wstage[k % 2], bf(f"wstage{k % 2}")
                    cv, cB = wcv[k % 2], bf(f"wcv{k % 2}")
                    sc.dma("sp", stg[:, 0:512], w_out[l, kt * 128:(kt + 1) * 128, c0:c0 + 512],
                           writes=[sB])
                    sc.op("pool", lambda h, stg=stg, cv=cv: h.tensor_copy(
                        out=cv[:, 0:512], in_=stg[:, 0:512]), reads=[sB], writes=[cB])
                    wb = Buf(f"wbo{l}_{k}")
                    wbo_bufs[l].append(wb)
                    sc.dma("pool", wbo_d[l, :, kt, c0:c0 + 512], cv[:, 0:512], reads=[cB], writes=[wb])
                    k += 1

        def layer_params(l):
            sc.dma("sp", psb[:, :], ps_d[:, l * 256:(l + 1) * 256], writes=[bf("psb")])
            sc.dma("sp", wp_st[:, :, :], w_pool[l].rearrange("g c d -> c g d"), writes=[bf("wp_st")])
            sc.op("pool", lambda h: h.tensor_tensor(
                out=wp_bf[:, :, :], in0=wp_st[:, :, :],
                in1=psb[0:64, :].rearrange("p (g d) -> p g d", g=4), op=ALU.mult),
                reads=[bf("wp_st"), bf("psb")], writes=[bf("wp_bf")])
            sc.op("pool", lambda h, l=l: h.tensor_tensor(
                out=gqk[:, :], in0=gq[:, l * 64:(l + 1) * 64], in1=gk[:, l * 64:(l + 1) * 64],
                op=ALU.mult), reads=[bf("gq"), bf("gk")], writes=[bf("gqk")])
            sc.op("pool", lambda h: h.tensor_scalar(out=gqk[:, :], in0=gqk[:, :], scalar1=0.125,
                                                    scalar2=None, op0=ALU.mult),
                  reads=[bf("gqk")], writes=[bf("gqk")])
            for hd in range(8):
                sc.op("pool", lambda h, hd=hd: h.tensor_copy(out=gqk8[:, hd * 64:(hd + 1) * 64], in_=gqk[:, :]),
                      reads=[bf("gqk")], writes=[bf("gqk8")])

        psn = {"a": 0, "o": 0, "c": 0}
        pools = {"a": [0, 1, 2, 3, 4, 5], "o": [6, 7], "z": [0, 1, 2, 3], "c": [4, 5]}

        def psum(kind="a"):
            key = "a" if kind == "z" else kind
            lst = pools[kind]
            i = lst[psn[key] % len(lst)]
            psn[key] += 1
            return PS[i], bf(f"ps{i}")

        def phase_a(l, x_src):
            xv = x_src.rearrange("(i p) d -> i p d", p=128)
            for j in range(DEBUG.get("nblk", NBLK)):
                for s in range(4):
                    i = 4 * j + s
                    xb, xB = xt[i % 2], bf(f"xt{i % 2}")
                    hb, hB = hn[i % 2], bf(f"hn{i % 2}")
                    sc.dma("sp", xb[:, :], xv[i], reads=[bf(f"x{l}_{i}")] if l else [], writes=[xB])
                    sc.op("act", lambda h, xb=xb, s=s: h.activation(
                        out=junk[:, :], in_=xb[:, :], func=AF.Square, accum_out=ss[:, s:s + 1]),
                        reads=[xB], writes=[bf("junk"), bf("ss")])
                    sc.op("act", lambda h, s=s: h.activation(
                        out=rstd[:, s:s + 1], in_=ss[:, s:s + 1], func=AF.Ln, scale=1.0 / D, bias=epsb[:, 0:1]),
                        reads=[bf("ss"), bf("epsb")], writes=[bf("rstd")])
                    sc.op("act", lambda h, s=s: h.activation(
                        out=rstd[:, s:s + 1], in_=rstd[:, s:s + 1], func=AF.Exp, scale=-0.5),
                        reads=[bf("rstd")], writes=[bf("rstd")])
                    sc.op("dve", lambda h, xb=xb, hb=hb, s=s: h.tensor_scalar(
                        out=hb[:, :], in0=xb[:, :], scalar1=rstd[:, s:s + 1], scalar2=None,
                        op0=ALU.mult), reads=[xB, bf("rstd")], writes=[hB])
                    pt, pB = psum()
                    ptb = pt[:, :].bitcast(BF16)
                    for kt in range(8):
                        sc.op("pe", lambda h, ptb=ptb, hb=hb, kt=kt: h.transpose(
                            out=ptb[:, kt * 128:(kt + 1) * 128], in_=hb[:, kt * 128:(kt + 1) * 128],
                            identity=ident), reads=[hB, bf("cb")], writes=[pB], inc=(kt == 7))
                    sc.op("act", lambda h, ptb=ptb, s=s: h.activation(
                        out=hT[:, :, s * 128:(s + 1) * 128],
                        in_=ptb.rearrange("p (k t) -> p k t", k=8), func=AF.Copy),
                        reads=[pB], writes=[bf("hT")])

                sc.dma("sp", wff[:, :, :], wbi_d[l, :, :, C_FF:C_FF + 8], reads=wbi_bufs[l],
                       writes=[bf("wff")])
                for s in range(4):
                    i = 4 * j + s
                    pt, pB = psum()
                    for kt in range(8):
                        sc.op("pe", lambda h, pt=pt, kt=kt, s=s: h.matmul(
                            pt[:, 0:8], lhsT=hT[:, kt, s * 128:(s + 1) * 128], rhs=wff[:, kt, :],
                            start=(kt == 0), stop=(kt == 7)),
                            reads=[bf("hT"), bf("wff")], writes=[pB], inc=(kt == 7))
                    v, av, ex, ln_, mn = (lf_t[:, k, :] for k in range(5))
                    sc.op("dve", lambda h, pt=pt, v=v, l=l: h.tensor_tensor(
                        out=v, in0=pt[:, 0:8], in1=bfb[:, l * 8:(l + 1) * 8], op=ALU.add),
                        reads=[pB, bf("bfb")], writes=[bf("lf_t")])
                    sc.op("act", lambda h, v=v, av=av: h.activation(out=av, in_=v, func=AF.Abs),
                          reads=[bf("lf_t")], writes=[bf("lf_t")])
                    sc.op("act", lambda h, av=av, ex=ex: h.activation(
                        out=ex, in_=av, func=AF.Exp, scale=-1.0),
                        reads=[bf("lf_t")], writes=[bf("lf_t")])
                    sc.op("act", lambda h, ex=ex, ln_=ln_: h.activation(
                        out=ln_, in_=ex, func=AF.Ln, bias=1.0),
                        reads=[bf("lf_t")], writes=[bf("lf_t")])
                    sc.op("dve", lambda h, v=v, mn=mn: h.tensor_scalar(
                        out=mn, in0=v, scalar1=0.0, scalar2=None, op0=ALU.min),
                        reads=[bf("lf_t")], writes=[bf("lf_t")])
                    sc.op("dve", lambda h, mn=mn, ln_=ln_: h.tensor_tensor(
                        out=lf[:, :], in0=mn, in1=ln_, op=ALU.subtract),
                        reads=[bf("lf_t")], writes=[bf("lf")])
                    r1, r2 = csp[:, 4, :], csp[:, 5, :]
                    sc.op("dve", lambda h: h.tensor_copy(out=lfs[:, 0, :], in_=lf[:, :]),
                          reads=[bf("lf")], writes=[bf("lfs")])
                    sc.op("dve", lambda h, r1=r1: h.tensor_tensor(
                        out=r1, in0=lf[:, :], in1=lfs[:, 0, :], op=ALU.subtract),
                        reads=[bf("lf"), bf("lfs")], writes=[bf("csp")])
                    sc.op("dve", lambda h, r1=r1: h.tensor_copy(out=lfs[:, 1, :], in_=r1),
                          reads=[bf("csp")], writes=[bf("lfs")])
                    sc.op("dve", lambda h, r1=r1, r2=r2: h.tensor_tensor(
                        out=r2, in0=r1, in1=lfs[:, 1, :], op=ALU.subtract),
                        reads=[bf("csp"), bf("lfs")], writes=[bf("csp")])
                    sc.op("dve", lambda h, r2=r2: h.tensor_copy(out=lfs[:, 2, :], in_=r2),
                          reads=[bf("csp")], writes=[bf("lfs")])
                    pc, pcB = psum()
                    nmm = 3 if i == 0 else 6
                    for k in range(3):
                        sc.op("pe", lambda h, pc=pc, k=k, nmm=nmm: h.matmul(
                            pc[:, 0:8], lhsT=Lmat, rhs=lfs[:, k, :], start=(k == 0), stop=(k == nmm - 1)),
                            reads=[bf("lfs"), bf("cb")], writes=[pcB], inc=(k == nmm - 1))
                    if i > 0:
                        for k in range(3):
                            sc.op("pe", lambda h, pc=pc, i=i, k=k: h.matmul(
                                pc[:, 0:8], lhsT=Esel, rhs=csb[:, i - 1, k, :], start=False, stop=(k == 2)),
                                reads=[bf("csb"), bf("cb")], writes=[pcB], inc=(k == 2))
                    sc.op("dve", lambda h, pc=pc, i=i: h.tensor_copy(out=c_sb[:, i, :], in_=pc[:, 0:8]),
                          reads=[pcB], writes=[bf("c_sb")])
                    sc.op("dve", lambda h, i=i: h.tensor_scalar(
                        out=negc[:, i, :], in0=c_sb[:, i, :], scalar1=-1.0, scalar2=None,
                        op0=ALU.mult), reads=[bf("c_sb")], writes=[bf("negc")])
                    t1, t2 = csp[:, 0, :], csp[:, 1, :]
                    sc.op("dve", lambda h, i=i: h.tensor_copy(out=csb[:, i, 0, :], in_=c_sb[:, i, :]),
                          reads=[bf("c_sb")], writes=[bf("csb")])
                    sc.op("dve", lambda h, i=i, t1=t1: h.tensor_tensor(
                        out=t1, in0=c_sb[:, i, :], in1=csb[:, i, 0, :], op=ALU.subtract),
                        reads=[bf("c_sb"), bf("csb")], writes=[bf("csp")])
                    sc.op("dve", lambda h, i=i, t1=t1: h.tensor_copy(out=csb[:, i, 1, :], in_=t1),
                          reads=[bf("csp")], writes=[bf("csb")])
                    sc.op("dve", lambda h, i=i, t1=t1, t2=t2: h.tensor_tensor(
                        out=t2, in0=t1, in1=csb[:, i, 1, :], op=ALU.subtract),
                        reads=[bf("csp"), bf("csb")], writes=[bf("csp")])
                    sc.op("dve", lambda h, i=i, t2=t2: h.tensor_copy(out=csb[:, i, 2, :], in_=t2),
                          reads=[bf("csp")], writes=[bf("csb")])

                for ci, (kind, c0) in enumerate(chunks):
                    if kind not in DEBUG.get("parts", "q k v g pp sqk svg").split():
                        continue
                    n = j * 7 + ci
                    gi = n % 2
                    wch = big[gi][:, :].rearrange("p (k c) -> p k c", k=8)
                    wB = bf(f"big{gi}")
                    load_chunk(l, n)
                    for s in range(4):
                        i = 4 * j + s
                        pt, pB = psum()
                        for kt in range(8):
                            sc.op("pe", lambda h, pt=pt, kt=kt, s=s, wch=wch: h.matmul(
                                pt[:, :], lhsT=hT[:, kt, s * 128:(s + 1) * 128], rhs=wch[:, kt, :],
                                start=(kt == 0), stop=(kt == 7)),
                                reads=[bf("hT"), wB], writes=[pB], inc=(kt == 7))
                        evac(l, j, s, i, kind, pt, pB)

        chunks = [("q", C_FQ), ("k", C_FK), ("v", C_FV), ("g", C_FG), ("pp", C_PX),
                  ("sqk", C_SQ), ("svg", C_SV)]

        def load_chunk(l, n):
            c0 = chunks[n % 7][1]
            gi = n % 2
            wch = big[gi][:, :].rearrange("p (k c) -> p k c", k=8)
            sc.dma("sp", wch, wbi_d[l, :, :, c0:c0 + 512], reads=wbi_bufs[l], writes=[bf(f"big{gi}")])

        def qk_norm(pt, pB, col):
            sc.op("act", lambda h, pt=pt: h.activation(out=sqtmp[:, :], in_=pt[:, :], func=AF.Square),
                  reads=[pB], writes=[bf("sqtmp")])
            sc.op("dve", lambda h, col=col: h.tensor_reduce(
                out=ssq[:, col:col + 8], in_=sqtmp[:, :].rearrange("p (h d) -> p h d", h=8),
                axis=AX.X, op=ALU.add), reads=[bf("sqtmp")], writes=[bf("ssq")])
            sc.op("act", lambda h, col=col: h.activation(
                out=rsq[:, col:col + 8], in_=ssq[:, col:col + 8], func=AF.Ln, scale=1.0 / 64, bias=epsb[:, 0:1]),
                reads=[bf("ssq"), bf("epsb")], writes=[bf("rsq")])
            sc.op("act", lambda h, col=col: h.activation(
                out=rsq[:, col:col + 8], in_=rsq[:, col:col + 8], func=AF.Exp, scale=-0.5),
                reads=[bf("rsq")], writes=[bf("rsq")])

        def evac(l, j, s, i, kind, pt, pB):
            half = s // 2
            so = (s % 2) * 128
            t0 = j * 512 + half * 256
            if kind == "q":
                qk_norm(pt, pB, 0)
                for hd in range(8):
                    sc.op("act", lambda h, pt=pt, hd=hd: h.activation(
                        out=qa[:, hd, 0:64], in_=pt[:, hd * 64:(hd + 1) * 64], func=AF.Copy,
                        scale=rsq[:, hd:hd + 1]), reads=[pB, bf("rsq")], writes=[bf("qa")])
                for k3 in range(3):
                    sc.op("dve", lambda h, i=i, k3=k3: h.tensor_copy(
                        out=qa[:, :, 64 + k3], in_=csb[:, i, k3, :]),
                        reads=[bf("csb")], writes=[bf("qa")])
                pq, pqB = psum()
                pqb = pq[:, :].bitcast(BF16)
                for hd in range(8):
                    sc.op("pe", lambda h, pqb=pqb, hd=hd: h.transpose(
                        out=pqb[0:67, hd * 128:(hd + 1) * 128], in_=qa[:, hd, 0:67], identity=ident),
                        reads=[bf("qa"), bf("cb")], writes=[pqB], inc=(hd == 7))
                sc.op("act", lambda h, pqb=pqb, so=so: h.activation(
                    out=QTst[0:67, :, so:so + 128],
                    in_=pqb[0:67, :].rearrange("p (k t) -> p k t", k=8), func=AF.Copy),
                    reads=[pqB], writes=[bf("QTst")])
                if s % 2 == 1:
                    sc.dma("sp", qt_d[:, :, t0:t0 + 256].rearrange("h r t -> r h t"),
                           QTst[0:67, :, :], reads=[bf("QTst")], writes=[bf(f"qt{j}")])
            elif kind == "k":
                qk_norm(pt, pB, 8)
                for hd in range(8):
                    sc.op("act", lambda h, pt=pt, hd=hd: h.activation(
                        out=ktmp[:, hd * 64:(hd + 1) * 64], in_=pt[:, hd * 64:(hd + 1) * 64],
                        func=AF.Copy, scale=rsq[:, 8 + hd:9 + hd]),
                        reads=[pB, bf("rsq")], writes=[bf("ktmp")])
                sc.op("dve", lambda h: h.tensor_tensor(
                    out=ka[:, :, :].rearrange("p h d -> p (h d)"), in0=ktmp[:, :], in1=gqk8[:, :],
                    op=ALU.mult), reads=[bf("ktmp"), bf("gqk8")], writes=[bf("ka")])
                pq, pqB = psum()
                pqb = pq[:, :].bitcast(BF16)
                for hd in range(8):
                    sc.op("pe", lambda h, pqb=pqb, hd=hd: h.transpose(
                        out=pqb[0:64, hd * 128:(hd + 1) * 128], in_=ka[:, hd, :], identity=ident),
                        reads=[bf("ka"), bf("cb")], writes=[pqB], inc=(hd == 7))
                sc.op("act", lambda h, pqb=pqb, so=so: h.activation(
                    out=KTst[0:64, :, so:so + 128],
                    in_=pqb[0:64, :].rearrange("p (k t) -> p k t", k=8), func=AF.Copy),
                    reads=[pqB], writes=[bf("KTst")])
                if s % 2 == 1:
                    sc.dma("sp", kt_d[:, :, t0:t0 + 256].rearrange("h r t -> r h t"),
                           KTst[0:67, :, :], reads=[bf("KTst")], writes=[bf(f"kt{j}")])
            elif kind == "v":
                sc.op("act", lambda h, pt=pt, s=s: h.activation(
                    out=Vst[:, :, s, 0:64], in_=pt[:, :].rearrange("p (h d) -> p h d", h=8),
                    func=AF.Copy), reads=[pB], writes=[bf("Vst")])
                if s == 3:
                    vfB[j] = []
                    for hh in range(8):
                        wb = Buf(f"vf{j}_{hh}")
                        vfB[j].append(wb)
                        sc.dma("sp", vf_d[:, hh, 4 * j:4 * j + 4, :].rearrange("p a b -> p (a b)"),
                               Vst[:, hh, :, :].rearrange("p a b -> p (a b)"),
                               reads=[bf("Vst")], writes=[wb])
            elif kind == "g":
                sc.op("act", lambda h, pt=pt, i=i: h.activation(
                    out=GM[:, i, 0:512], in_=pt[:, :], func=AF.Silu),
                    reads=[pB], writes=[bf(f"GM{i}")])
            elif kind == "pp":
                slot = 1 + s
                if s == 0 and j > 0:
                    sc.op("dve", lambda h: h.tensor_copy(out=pxt[:, 0, :], in_=pxt[:, 4, :]),
                          reads=[bf("pxt")], writes=[bf("pxt")])
                sc.op("act", lambda h, pt=pt, slot=slot: h.activation(
                    out=pxt[:, slot, :], in_=pt[:, 0:256], func=AF.Copy), reads=[pB], writes=[bf("pxt")])
                sc.op("act", lambda h, pt=pt, i=i: h.activation(
                    out=GM[:, i, 512:768], in_=pt[:, 256:512], func=AF.Silu),
                    reads=[pB], writes=[bf(f"GM{i}")])
                pp, ppB = psum()
                for g in range(4):
                    base = CB_POOL + g * 3 * 128
                    m_same = cb[:, base + (256 if i == 0 else 0):base + (256 if i == 0 else 0) + 128]
                    m_prev = cb[:, base + 128:base + 256]
                    sc.op("pe", lambda h, pp=pp, g=g, slot=slot, m_same=m_same, i=i: h.matmul(
                        pp[0:64, g * 128:(g + 1) * 128], lhsT=pxt[:, slot, g * 64:(g + 1) * 64],
                        rhs=m_same, start=True, stop=(i == 0)),
                        reads=[bf("pxt"), bf("cb")], writes=[ppB], inc=(i == 0 and g == 3))
                    if i > 0:
                        sc.op("pe", lambda h, pp=pp, g=g, slot=slot, m_prev=m_prev: h.matmul(
                            pp[0:64, g * 128:(g + 1) * 128], lhsT=pxt[:, slot - 1, g * 64:(g + 1) * 64],
                            rhs=m_prev, start=False, stop=True),
                            reads=[bf("pxt"), bf("cb")], writes=[ppB], inc=(g == 3))
                sc.op("act", lambda h, pp=pp: h.activation(
                    out=pooledT[:, :, :].rearrange("p g t -> p (g t)"), in_=pp[0:64, :], func=AF.Copy),
                    reads=[ppB], writes=[bf("pooledT")])
                py, pyB = psum()
                for g in range(4):
                    sc.op("pe", lambda h, py=py, g=g: h.matmul(
                        py[:, g * 64:(g + 1) * 64], lhsT=pooledT[:, g, :], rhs=wp_bf[:, g, :],
                        start=True, stop=True),
                        reads=[bf("pooledT"), bf("wp_bf")], writes=[pyB], inc=(g == 3))
                sc.op("act", lambda h, py=py: h.activation(out=ytmp[:, :], in_=py[:, 0:256], func=AF.Copy),
                      reads=[pyB], writes=[bf("ytmp")])
                sc.op("dve", lambda h, i=i: h.tensor_tensor(
                    out=GM[:, i, 512:768], in0=ytmp[:, :], in1=GM[:, i, 512:768], op=ALU.mult),
                    reads=[bf("ytmp"), bf(f"GM{i}")], writes=[bf(f"GM{i}")])
            elif kind == "sqk":
                sc.op("act", lambda h, pt=pt: h.activation(
                    out=sqk[:, 0:256], in_=pt[:, 0:256], func=AF.Copy, scale=0.125),
                    reads=[pB], writes=[bf("sqk")])
                sc.op("act", lambda h, pt=pt: h.activation(out=sqk[:, 256:512], in_=pt[:, 256:512], func=AF.Copy),
                      reads=[pB], writes=[bf("sqk")])
                pq, pqB = psum()
                pqb = pq[:, :].bitcast(BF16)
                for k in range(4):
                    sc.op("pe", lambda h, pqb=pqb, k=k: h.transpose(
                        out=pqb[:, k * 128:(k + 1) * 128], in_=sqk[:, k * 128:(k + 1) * 128],
                        identity=ident), reads=[bf("sqk"), bf("cb")], writes=[pqB], inc=(k == 3))
                sc.op("act", lambda h, pqb=pqb, so=so: h.activation(
                    out=QsTst[:, :, so:so + 128],
                    in_=pqb[:, 0:256].rearrange("p (k t) -> p k t", k=2), func=AF.Copy),
                    reads=[pqB], writes=[bf("QsTst")])
                sc.op("act", lambda h, pqb=pqb, so=so: h.activation(
                    out=KsTst[:, :, so:so + 128],
                    in_=pqb[:, 256:512].rearrange("p (k t) -> p k t", k=2), func=AF.Copy),
                    reads=[pqB], writes=[bf("KsTst")])
                if s % 2 == 1:
                    sc.dma("sp", qs_d[:, :, t0:t0 + 256].rearrange("(a b) r t -> (b r) a t", b=2),
                           QsTst[:, :, :], reads=[bf("QsTst")], writes=[bf(f"qs{j}")])
                    sc.dma("sp", ks_d[:, :, t0:t0 + 256].rearrange("(a b) r t -> (b r) a t", b=2),
                           KsTst[:, :, :], reads=[bf("KsTst")], writes=[bf(f"ks{j}")])
            elif kind == "svg":
                sc.op("act", lambda h, pt=pt, s=s: h.activation(
                    out=sVst[:, :, s, :], in_=pt[:, 0:256].rearrange("p (h d) -> p h d", h=4),
                    func=AF.Copy), reads=[pB], writes=[bf("sVst")])
                sc.op("act", lambda h, pt=pt, i=i: h.activation(
                    out=GM[:, i, 768:1024], in_=pt[:, 256:512], func=AF.Silu),
                    reads=[pB], writes=[bf(f"GM{i}")])
                if s == 3:
                    vsB[j] = []
                    for hh in range(4):
                        wb = Buf(f"vs{j}_{hh}")
                        vsB[j].append(wb)
                        sc.dma("sp", vs_d[:, hh, 4 * j:4 * j + 4, :].rearrange("p a b -> p (a b)"),
                               sVst[:, hh, :, :].rearrange("p a b -> p (a b)"),
                               reads=[bf("sVst")], writes=[wb])

        def load_kv(kind, hd):
            KT, KB = big[hd % 2], bf(f"big{hd % 2}")
            Vb, VB = Vh[hd % 2], bf(f"Vh{hd % 2}")
            if kind == "f":
                sc.dma("sp", KT[0:67, :], kt_d[hd, :, :], reads=[bf(f"kt{j}") for j in range(NBLK)],
                       writes=[KB])
                sc.dma("sp", Vb[:, :, :].rearrange("p a b -> p (a b)"),
                       vf_d[:, hd, :, :].rearrange("p a b -> p (a b)"),
                       reads=[w for j in range(NBLK) for w in vfB[j]], writes=[VB])
            else:
                sc.dma("sp", KT[0:64, :], ks_d[hd, :, :], reads=[bf(f"ks{j}") for j in range(NBLK)],
                       writes=[KB])
                sc.dma("sp", Vb[:, :, 0:64], vs_d[:, hd, :, :],
                       reads=[w for j in range(NBLK) for w in vsB[j]], writes=[VB])

        def load_q(kind, hd, qb, slot):
            Qb, QB = QTb[slot], bf(f"QTb{slot}")
            if kind == "f":
                sc.dma("sp", Qb[0:67, :], qt_d[hd, :, qb * 512:(qb + 1) * 512],
                       reads=[bf(f"qt{qb}")], writes=[QB])
            else:
                sc.dma("sp", Qb[0:64, :], qs_d[hd, :, qb * 512:(qb + 1) * 512],
                       reads=[bf(f"qs{qb}")], writes=[QB])

        def phase_c_fox(l):
            load_kv("f", 0)
            load_q("f", 0, 0, 0)
            qn = 0
            for hd in range(8):
                KT, KB = big[hd % 2], bf(f"big{hd % 2}")
                Vb, VB = Vh[hd % 2], bf(f"Vh{hd % 2}")
                for qb in range(NBLK):
                    Qb, QB = QTb[qn % 2], bf(f"QTb{qn % 2}")
                    if qb + 1 < NBLK:
                        load_q("f", hd, qb + 1, (qn + 1) % 2)
                    elif hd + 1 < 8:
                        load_kv("f", hd + 1)
                        load_q("f", hd + 1, 0, (qn + 1) % 2)
                    qn += 1
                    po, poB = psum("o")
                    pov = po[:, 0:260].rearrange("p (s d) -> p s d", s=4)
                    nk = 4 * (qb + 1)
                    st_ = {}

                    def stage_a(ki):
                        o = ki - 4 * qb
                        pss, psB = psum("a")
                        sc.op("pe", lambda h, pss=pss, ki=ki, o=o, KT=KT, Qb=Qb: h.matmul(
                            pss[:, :], lhsT=KT[0:67, ki * 128:(ki + 1) * 128], rhs=Qb[0:67, :],
                            start=True, stop=(o < 0)), reads=[KB, QB], writes=[psB], inc=(o < 0))
                        if o >= 0:
                            mb = cb[:, CB_MBF + o * 512:CB_MBF + (o + 1) * 512]
                            sc.op("pe", lambda h, pss=pss, mb=mb: h.matmul(
                                pss[:, :], lhsT=ident, rhs=mb, start=False, stop=True),
                                reads=[bf("cb")], writes=[psB])
                        P, PB = Pt[ki % 3], bf(f"Pt{ki % 3}")
                        sc.op("act", lambda h, pss=pss, P=P, ki=ki, hd=hd: h.activation(
                            out=P[:, :], in_=pss[:, :], func=AF.Exp, bias=negc[:, ki, hd:hd + 1]),
                            reads=[psB, bf("negc")], writes=[PB])
                        st_[ki] = (P, PB)

                    def stage_c(ki):
                        P, PB = st_.pop(ki)
                        sc.op("pe", lambda h, P=P, ki=ki, po=po, Vb=Vb, nk=nk: h.matmul(
                            po[0:65, :], lhsT=Vb[:, ki, :], rhs=P[:, :],
                            start=(ki == 0), stop=(ki == nk - 1)),
                            reads=[PB, VB], writes=[poB], inc=True)

                    for k in range(nk + 1):
                        if k < nk:
                            stage_a(k)
                        if k >= 1:
                            stage_c(k - 1)
                    ob, oB = osb[qn % 2], bf(f"osb{qn % 2}")
                    sc.op("act", lambda h, po=po, ob=ob: h.activation(
                        out=ob[0:65, :], in_=po[0:65, :], func=AF.Copy), reads=[poB], writes=[oB])
                    p2, p2B = psum("a")
                    p2v = p2[:, 0:260].rearrange("p (s d) -> p s d", s=4)
                    for s4 in range(4):
                        sc.op("pe", lambda h, p2v=p2v, ob=ob, s4=s4: h.transpose(
                            out=p2v[:, s4, :], in_=ob[0:65, s4 * 128:(s4 + 1) * 128],
                            identity=identf[0:65, 0:65]), reads=[oB, bf("cf")], writes=[p2B],
                            inc=(s4 == 3))
                    sc.op("act", lambda h, p2=p2: h.activation(out=o2[:, :], in_=p2[:, 0:260], func=AF.Copy),
                          reads=[p2B], writes=[bf("o2")])
                    o2v = o2[:, :].rearrange("p (s d) -> p s d", s=4)
                    sc.op("dve", lambda h, o2v=o2v: h.reciprocal(out=rec[:, :], in_=o2v[:, :, 64]),
                          reads=[bf("o2")], writes=[bf("rec")])
                    for s4 in range(4):
                        i = 4 * qb + s4
                        sc.op("dve", lambda h, o2v=o2v, s4=s4: h.tensor_scalar(
                            out=o3[:, s4 * 64:(s4 + 1) * 64], in0=o2v[:, s4, 0:64],
                            scalar1=rec[:, s4:s4 + 1], scalar2=None, op0=ALU.mult),
                            reads=[bf("o2"), bf("rec")], writes=[bf("o3")])
                        sc.op("dve", lambda h, s4=s4, i=i, hd=hd: h.tensor_tensor(
                            out=GM[:, i, hd * 64:(hd + 1) * 64], in0=o3[:, s4 * 64:(s4 + 1) * 64],
                            in1=GM[:, i, hd * 64:(hd + 1) * 64], op=ALU.mult),
                            reads=[bf("o3"), bf(f"GM{i}")], writes=[bf(f"GM{i}")])

        def phase_c_sb(l):
            uneg = cb[:, CB_UNEG:CB_UNEG + 128]
            oneg = cb[:, CB_ONEG:CB_ONEG + 128]
            load_kv("s", 0)
            load_q("s", 0, 0, 0)
            qn = 0
            for hd in range(4):
                KT, KB = big[hd % 2], bf(f"big{hd % 2}")
                Vb, VB = Vh[hd % 2], bf(f"Vh{hd % 2}")
                Vv = Vb[:, :, 0:64]
                for qb in range(NBLK):
                    Qb, QB = QTb[qn % 2], bf(f"QTb{qn % 2}")
                    if qb + 1 < NBLK:
                        load_q("s", hd, qb + 1, (qn + 1) % 2)
                    elif hd + 1 < 4:
                        load_kv("s", hd + 1)
                        load_q("s", hd + 1, 0, (qn + 1) % 2)
                    qn += 1
                    po, poB = psum("o")
                    pov = po[:, 0:256].rearrange("p (s d) -> p s d", s=4)
                    nk = 4 * (qb + 1)
                    st_ = {}

                    def stage_a(n):
                        ki = nk - 1 - n
                        o = ki - 4 * qb
                        pz, pzB = psum("z")
                        sc.op("pe", lambda h, pz=pz, ki=ki, o=o, KT=KT, Qb=Qb: h.matmul(
                            pz[:, :], lhsT=KT[0:64, ki * 128:(ki + 1) * 128], rhs=Qb[0:64, :],
                            start=True, stop=(o < 0)), reads=[KB, QB], writes=[pzB], inc=(o < 0))
                        if o >= 0:
                            mb = cb[:, CB_MBS + o * 512:CB_MBS + (o + 1) * 512]
                            sc.op("pe", lambda h, pz=pz, mb=mb: h.matmul(
                                pz[:, :], lhsT=ident, rhs=mb, start=False, stop=True),
                                reads=[bf("cb")], writes=[pzB])
                        eb, eB = e_sb[n % 2], bf(f"e_sb{n % 2}")
                        spb, spB = sp_sb[n % 2], bf(f"sp_sb{n % 2}")
                        sc.op("act", lambda h, pz=pz, eb=eb: h.activation(
                            out=eb[:, :], in_=pz[:, :], func=AF.Exp), reads=[pzB], writes=[eB])
                        sc.op("act", lambda h, eb=eb, spb=spb: h.activation(
                            out=spb[:, :], in_=eb[:, :], func=AF.Ln, bias=1.0),
                            reads=[eB], writes=[spB])
                        st_[n] = dict(ki=ki, pz=pz, pzB=pzB, spb=spb, spB=spB)

                    def stage_b(n):
                        d = st_[n]
                        pz, pzB, spb, spB, ki = d["pz"], d["pzB"], d["spb"], d["spB"], d["ki"]
                        sc.op("pe", lambda h, pz=pz, spb=spb: h.matmul(
                            pz[:, :], lhsT=uneg, rhs=spb[:, :], start=False, stop=True,
                            skip_group_check=True),
                            reads=[spB, bf("cb")], writes=[pzB])
                        pcs, pcsB = psum("c")
                        sc.op("pe", lambda h, pcs=pcs, spb=spb: h.matmul(
                            pcs[:, :], lhsT=oneg, rhs=spb[:, :], start=True, stop=True),
                            reads=[spB, bf("cb")], writes=[pcsB])
                        ab, aB = arg_sb[n % 2], bf(f"arg_sb{n % 2}")
                        P, PB = Pt[n % 3], bf(f"Pt{n % 3}")
                        if n == 0:
                            sc.op("act", lambda h, pz=pz, P=P: h.activation(
                                out=P[:, :], in_=pz[:, :], func=AF.Exp), reads=[pzB], writes=[PB])
                            sc.op("dve", lambda h, pcs=pcs: h.tensor_copy(out=carry[:, :], in_=pcs[:, :]),
                                  reads=[pcsB], writes=[bf("carry")])
                        else:
                            sc.op("dve", lambda h, pz=pz, ab=ab: h.tensor_tensor(
                                out=ab[:, :], in0=pz[:, :], in1=carry[:, :], op=ALU.add),
                                reads=[pzB, bf("carry")], writes=[aB])
                            sc.op("act", lambda h, ab=ab, P=P: h.activation(
                                out=P[:, :], in_=ab[:, :], func=AF.Exp), reads=[aB], writes=[PB])
                            if ki > 0:
                                sc.op("dve", lambda h, pcs=pcs: h.tensor_tensor(
                                    out=carry[:, :], in0=pcs[:, :], in1=carry[:, :], op=ALU.add),
                                    reads=[pcsB, bf("carry")], writes=[bf("carry")])
                        d["P"], d["PB"] = P, PB

                    def stage_c(n):
                        d = st_.pop(n)
                        P, PB, ki = d["P"], d["PB"], d["ki"]
                        sc.op("pe", lambda h, P=P, ki=ki, n=n, po=po, Vv=Vv, nk=nk: h.matmul(
                            po[0:64, :], lhsT=Vv[:, ki, :], rhs=P[:, :],
                            start=(n == 0), stop=(n == nk - 1)),
                            reads=[PB, VB], writes=[poB], inc=True)

                    for k in range(nk + 2):
                        if k < nk:
                            stage_a(k)
                        if 1 <= k <= nk:
                            stage_b(k - 1)
                        if k >= 2:
                            stage_c(k - 2)
                    ob, oB = osb[qn % 2], bf(f"osb{qn % 2}")
                    sc.op("dve", lambda h, po=po, ob=ob: h.tensor_copy(out=ob[0:64, :], in_=po[0:64, :]),
                          reads=[poB], writes=[oB])
                    p2, p2B = psum("c")
                    p2v = p2[:, 0:256].rearrange("p (s d) -> p s d", s=4)
                    for s4 in range(4):
                        sc.op("pe", lambda h, p2v=p2v, ob=ob, s4=s4: h.transpose(
                            out=p2v[:, s4, :], in_=ob[0:64, s4 * 128:(s4 + 1) * 128],
                            identity=identf[0:64, 0:64]), reads=[oB, bf("cf")], writes=[p2B],
                            inc=(s4 == 3))
                    c0 = 768 + hd * 64
                    sc.op("act", lambda h, p2=p2: h.activation(out=o3[:, :], in_=p2[:, 0:256], func=AF.Copy),
                          reads=[p2B], writes=[bf("o3")])
                    for s4 in range(4):
                        i = 4 * qb + s4
                        sc.op("dve", lambda h, s4=s4, i=i, c0=c0: h.tensor_tensor(
                            out=GM[:, i, c0:c0 + 64], in0=o3[:, s4 * 64:(s4 + 1) * 64],
                            in1=GM[:, i, c0:c0 + 64], op=ALU.mult),
                            reads=[bf("o3"), bf(f"GM{i}")], writes=[bf(f"GM{i}")])

        def phase_d(l, x_src, x_dst, last):
            xv = x_src.rearrange("(i p) d -> i p d", p=128)
            ov = x_dst.rearrange("(i p) d -> i p d", p=128)
            wo = [big[k][:, :].rearrange("p (k c) -> p k c", k=8) for k in range(2)]
            for k in range(2):
                sc.dma("sp", wo[k], wbo_d[l, :, :, k * 512:(k + 1) * 512], reads=wbo_bufs[l],
                       writes=[bf(f"big{k}")])
            toks = []
            for i in range(NT):
                xb, xB = xt[i % 2], bf(f"xt{i % 2}")
                ob, oB = xo[i % 2], bf(f"xo{i % 2}")
                sc.dma("sp", xb[:, :], xv[i], reads=[bf(f"x{l}_{i}")] if l else [], writes=[xB])
                pt, pB = psum()
                ptb = pt[:, :].bitcast(BF16)
                for et in range(8):
                    sc.op("pe", lambda h, ptb=ptb, i=i, et=et: h.transpose(
                        out=ptb[:, et * 128:(et + 1) * 128], in_=GM[:, i, et * 128:(et + 1) * 128],
                        identity=ident), reads=[bf(f"GM{i}"), bf("cb")], writes=[pB], inc=(et == 7))
                sc.op("act", lambda h, ptb=ptb: h.activation(
                    out=hT[:, :, 0:128], in_=ptb.rearrange("p (k t) -> p k t", k=8), func=AF.Copy),
                    reads=[pB], writes=[bf("hT")])
                for k in range(2):
                    py, pyB = psum()
                    for et in range(8):
                        sc.op("pe", lambda h, py=py, et=et, k=k: h.matmul(
                            py[:, :], lhsT=hT[:, et, 0:128], rhs=wo[k][:, et, :],
                            start=(et == 0), stop=(et == 7)),
                            reads=[bf("hT"), bf(f"big{k}")], writes=[pyB], inc=(et == 7))
                    sc.op("dve", lambda h, py=py, ob=ob, xb=xb, k=k: h.tensor_tensor(
                        out=ob[:, k * 512:(k + 1) * 512], in0=py[:, :], in1=xb[:, k * 512:(k + 1) * 512],
                        op=ALU.add), reads=[pyB, xB], writes=[oB])
                toks.append(sc.dma("sp", ov[i], ob[:, :], reads=[oB], writes=[bf(f"x{l + 1}_{i}")]))
            return toks

        convert_weights(0)
        toks = []
        for l in range(depth):
            layer_params(l)
            x_src = x_in if l == 0 else xs_d[(l - 1) % 2]
            last = (l == depth - 1)
            x_dst = out_d if last else xs_d[l % 2]
            ph = DEBUG.get("phases", "afsd")
            if "a" in ph:
                phase_a(l, x_src)
            if l + 1 < depth:
                convert_weights(l + 1)
            if "f" in ph:
                phase_c_fox(l)
            if "s" in ph:
                phase_c_sb(l)
            if "d" in ph:
                toks = phase_d(l, x_src, x_dst, last)
        if dump:
            toks.append(sc.dma("sp", dbg["qt"][:, :, :], qt_d[:, :, :],
                               reads=[bf(f"qt{j}") for j in range(NBLK)]))
            toks.append(sc.dma("sp", dbg["kt"][:, :, :], kt_d[:, :, :],
                               reads=[bf(f"kt{j}") for j in range(NBLK)]))
            toks.append(sc.dma("sp", dbg["gm"].rearrange("(i p) d -> p i d", p=128), GM[:, :, :],
                               reads=[bf(f"GM{i}") for i in range(NT)]))
            toks.append(sc.dma("sp", dbg["c"], c_sb[:, :, :].rearrange("p i h -> p (i h)"),
                               reads=[bf("c_sb")]))
        sc.finish("sp", toks)

        with nc.Block() as block:
            sc.replay(block)
    return nc


_CACHE = {}


def kernel(x, norm_g, w_in, b_f, q_norm_g, k_norm_g, w_pool, pool_scale, w_out):
    depth = DEBUG.get("depth", DEPTH)
    dump = DEBUG.get("dump", False)
    x = np.ascontiguousarray(np.asarray(x, np.float32))
    f32 = lambda a: np.ascontiguousarray(np.asarray(a, np.float32))
    cbh, cfh = _host_consts()
    gcol = f32(np.asarray(norm_g, np.float32).reshape(DEPTH, 8, 128).transpose(2, 0, 1).reshape(128, DEPTH * 8))
    gq_bc = f32(np.broadcast_to(np.asarray(q_norm_g, np.float32).reshape(1, DEPTH * 64), (128, DEPTH * 64)))
    gk_bc = f32(np.broadcast_to(np.asarray(k_norm_g, np.float32).reshape(1, DEPTH * 64), (128, DEPTH * 64)))
    bf_bc = f32(np.broadcast_to(np.asarray(b_f, np.float32).reshape(1, DEPTH * 8), (128, DEPTH * 8)))
    ps_bc = f32(np.broadcast_to(np.asarray(pool_scale, np.float32).reshape(1, DEPTH * 256), (128, DEPTH * 256)))
    key = (depth, dump)
    if key not in _CACHE:
        _CACHE[key] = build_program(depth, dump)
    nc = _CACHE[key]
    shared = dict(w_in=f32(w_in), w_out=f32(w_out), w_pool=f32(w_pool), gcol=gcol, gq_bc=gq_bc,
                  gk_bc=gk_bc, bf_bc=bf_bc, ps_bc=ps_bc, cb=cbh, cf=cfh)
    in_maps = []
    for c in range(8):
        m = dict(shared)
        m["x"] = x[c % 4]
        in_maps.append(m)
    if DEBUG.get("fake_inputs"):
        for m in in_maps:
            for k in ("x", "w_in", "w_out"):
                m.pop(k)
    res = run_bass_kernel_spmd(nc, in_maps, core_ids=list(range(8)))
    out = np.stack([np.asarray(res.results[b]["out"], np.float32) for b in range(4)], axis=0)
    if dump:
        DEBUG["res"] = res.results
    return out
```

```python
import contextlib
import numpy as np
import ml_dtypes
import concourse.bass as bass
import concourse.mybir as mybir
from concourse.bass_utils import run_bass_kernel_spmd

F32 = mybir.dt.float32
BF16 = mybir.dt.bfloat16
AF = mybir.ActivationFunctionType
ALU = mybir.AluOpType
AX = mybir.AxisListType

S = 4096
D = 1024
NT = 32
NBLK = 8
DIN = 3592
DEPTH = 4
EPS = 1e-6
NEGBIG = -30000.0

C_FQ, C_FK, C_FV, C_FG, C_FF, C_PX, C_PG, C_SQ, C_SK, C_SV, C_SG = (
    0, 512, 1024, 1536, 2048, 2056, 2312, 2568, 2824, 3080, 3336)

CB_ID = 0
CB_UNEG = 128
CB_ONEG = 256
CB_MBF = 384
CB_MBS = CB_MBF + 4 * 512
CB_POOL = CB_MBS + 4 * 512
CB_L = CB_POOL + 12 * 128
CB_E = CB_L + 128
CB_N = CB_E + 128
CF_L = 0
CF_E = 128
CF_ID = 256
CF_N = 384

DEBUG = {}


class Buf:
    __slots__ = ("name", "w", "r")

    def __init__(self, name):
        self.name = name
        self.w = None
        self.r = []


class Sched:
    def __init__(self, nc, stack):
        self.nc = nc
        self.sems = {}
        self.E = {}
        for name, h in (("pe", nc.tensor), ("act", nc.scalar), ("dve", nc.vector),
                        ("pool", nc.gpsimd), ("sp", nc.sync)):
            self.sems["s_" + name] = stack.enter_context(nc.semaphore("s_" + name))
            self.E[name] = dict(h=h, cnt=0, ops=[], known={}, dn=0, dcnt=[])
        for name, k in (("sp", 28), ("pool", 12), ("act", 8)):
            for i in range(k):
                self.sems[f"d_{name}{i}"] = stack.enter_context(nc.semaphore(f"d_{name}{i}"))
                self.E[name]["dcnt"].append(0)

    def _emit(self, en, fn, reads, writes, inc=True, dsem=None, dval=None, extra=()):
        e = self.E[en]
        need = {}

        def add(tok, raw):
            if tok is None:
                return
            sname, val, src = tok
            if src == en and en == "pe":
                return
            if need.get(sname, 0) < val:
                need[sname] = val

        for b in reads:
            add(b.w, True)
        for b in writes:
            add(b.w, False)
            for t in b.r:
                add(t, False)
        for sname, val in extra:
            if need.get(sname, 0) < val:
                need[sname] = val
        waits = []
        for sname, val in need.items():
            if e["known"].get(sname, 0) < val:
                e["known"][sname] = val
                waits.append((sname, val))
        if dsem is None:
            tok = ("s_" + en, e["cnt"] + 1, en)
            if inc:
                e["cnt"] += 1
        else:
            tok = (dsem, dval, "dma:" + en)
        e["ops"].append((waits, fn, inc, dsem))
        for b in writes:
            b.w = tok
            b.r = []
        for b in reads:
            b.r = [t for t in b.r if t[2] != tok[2] or tok[2].startswith("dma")] + [tok]
        return tok

    def op(self, en, fn, reads=(), writes=(), inc=True):
        return self._emit(en, fn, list(reads), list(writes), inc=inc)

    def dma(self, q, out, in_, reads=(), writes=()):
        e = self.E[q]
        i = e["dn"] % len(e["dcnt"])
        e["dn"] += 1
        sname = f"d_{q}{i}"
        prev = 16 * e["dcnt"][i]
        e["dcnt"][i] += 1
        val = 16 * e["dcnt"][i]
        fn = lambda h, out=out, in_=in_: h.dma_start(out=out, in_=in_)
        return self._emit(q, fn, list(reads), list(writes), inc=False, dsem=sname, dval=val,
                          extra=[(sname, prev)] if prev else [])

    def finish(self, en, toks):
        extra = [(t[0], t[1]) for t in toks]
        self._emit(en, None, [], [], inc=False, extra=extra)
        for q, e in self.E.items():
            ex = [(f"d_{q}{i}", 16 * c) for i, c in enumerate(e["dcnt"]) if c]
            if ex:
                self._emit(q, None, [], [], inc=False, extra=ex)

    def replay(self, block):
        sems = self.sems

        def run(en, h):
            own = sems["s_" + en]
            for waits, fn, inc, dsem in self.E[en]["ops"]:
                for sname, val in waits:
                    h.wait_ge(sems[sname], val)
                if fn is None:
                    continue
                ins = fn(h)
                if dsem is not None:
                    ins.then_inc(sems[dsem], 16)
                elif inc:
                    ins.then_inc(own, 1)

        @block.tensor
        def _(h):
            run("pe", h)

        @block.scalar
        def _(h):
            run("act", h)

        @block.vector
        def _(h):
            run("dve", h)

        @block.gpsimd
        def _(h):
            run("pool", h)

        @block.sync
        def _(h):
            run("sp", h)


def _host_consts():
    p = np.arange(128)[:, None]
    f = np.arange(512)[None, :]
    f128 = np.arange(128)[None, :]
    cb = np.zeros((128, CB_N), np.float32)
    cb[:, CB_ID:CB_ID + 128] = np.eye(128)
    cb[:, CB_UNEG:CB_UNEG + 128] = -1.0 * (p >= f128)
    cb[:, CB_ONEG:CB_ONEG + 128] = -1.0
    for o in range(4):
        cb[:, CB_MBF + o * 512:CB_MBF + (o + 1) * 512] = np.where(f >= p + 128 * o, 0.0, NEGBIG)
        cb[:, CB_MBS + o * 512:CB_MBS + (o + 1) * 512] = np.where(f > p + 128 * o, 0.0, NEGBIG)
    for g, w in enumerate((2, 4, 8, 16)):
        s_ = p
        t_ = f128
        same = np.where((s_ <= t_) & (s_ > t_ - w), 1.0 / w, 0.0) - 1.0 * (s_ == t_)
        prev = np.where(s_ - 128 > t_ - w, 1.0 / w, 0.0)
        cnt = np.minimum(t_ + 1, w)
        first = np.where((s_ <= t_) & (s_ > t_ - w), 1.0 / cnt, 0.0) - 1.0 * (s_ == t_)
        base = CB_POOL + (g * 3) * 128
        cb[:, base:base + 128] = same
        cb[:, base + 128:base + 256] = prev
        cb[:, base + 256:base + 384] = first
    cb[:, CB_L:CB_L + 128] = 1.0 * (p <= f128)
    cb[:, CB_E:CB_E + 128] = 1.0 * ((p == 127) & (f128 >= 0))
    cf = np.zeros((128, CF_N), np.float32)
    cf[:, CF_L:CF_L + 128] = 1.0 * (p <= f128)
    cf[:, CF_E:CF_E + 128] = 1.0 * ((p == 127) & (f128 >= 0))
    cf[:, CF_ID:CF_ID + 128] = np.eye(128)
    return cb.astype(ml_dtypes.bfloat16), cf


def build_program(depth=DEPTH, dump=False):
    nc = bass.Bass("TRN2", target_bir_lowering=False)
    dt = nc.dram_tensor
    bigkind = "Internal" if DEBUG.get("fake_inputs") else "ExternalInput"
    x_in = dt("x", [S, D], F32, kind=bigkind).ap()
    w_in = dt("w_in", [DEPTH, D, DIN], F32, kind=bigkind).ap()
    w_out = dt("w_out", [DEPTH, D, D], F32, kind=bigkind).ap()
    w_pool = dt("w_pool", [DEPTH, 4, 64, 64], F32, kind="ExternalInput").ap()
    gcol_d = dt("gcol", [128, DEPTH * 8], F32, kind="ExternalInput").ap()
    gq_d = dt("gq_bc", [128, DEPTH * 64], F32, kind="ExternalInput").ap()
    gk_d = dt("gk_bc", [128, DEPTH * 64], F32, kind="ExternalInput").ap()
    bf_d = dt("bf_bc", [128, DEPTH * 8], F32, kind="ExternalInput").ap()
    ps_d = dt("ps_bc", [128, DEPTH * 256], F32, kind="ExternalInput").ap()
    cb_d = dt("cb", [128, CB_N], BF16, kind="ExternalInput").ap()
    cf_d = dt("cf", [128, CF_N], F32, kind="ExternalInput").ap()
    out_d = dt("out", [S, D], F32, kind="ExternalOutput").ap()

    xs_d = [dt("xs0", [S, D], F32).ap(), dt("xs1", [S, D], F32).ap()]
    wbi_d = dt("wbi", [DEPTH, 128, 8, DIN], BF16).ap()
    wbo_d = dt("wbo", [DEPTH, 128, 8, D], BF16).ap()
    qt_d = dt("qt", [8, 67, S], BF16).ap()
    kt_d = dt("kt", [8, 67, S], BF16).ap()
    vf_d = dt("vf", [128, 8, NT, 65], BF16).ap()
    qs_d = dt("qs", [4, 64, S], BF16).ap()
    ks_d = dt("ks", [4, 64, S], BF16).ap()
    vs_d = dt("vs", [128, 4, NT, 64], BF16).ap()
    dbg = {}
    if dump:
        dbg["qt"] = dt("dbg_qt", [8, 67, S], BF16, kind="ExternalOutput").ap()
        dbg["kt"] = dt("dbg_kt", [8, 67, S], BF16, kind="ExternalOutput").ap()
        dbg["gm"] = dt("dbg_gm", [S, D], BF16, kind="ExternalOutput").ap()
        dbg["c"] = dt("dbg_c", [128, NT * 8], F32, kind="ExternalOutput").ap()

    with contextlib.ExitStack() as st:
        sb = lambda name, shape, dtype: st.enter_context(nc.sbuf_tensor("sb_" + name, shape, dtype))
        sc = Sched(nc, st)

        GM = sb("GM", [128, NT, D], BF16)
        cb = sb("cb", [128, CB_N], BF16)
        cf = sb("cf", [128, CF_N], F32)
        gcol = sb("gcol", [128, DEPTH * 8], F32)
        gq = sb("gq", [128, DEPTH * 64], F32)
        gk = sb("gk", [128, DEPTH * 64], F32)
        gqk = sb("gqk", [128, 64], F32)
        gqk8 = sb("gqk8", [128, 512], F32)
        bfb = sb("bfb", [128, DEPTH * 8], F32)
        psb = sb("psb", [128, 256], F32)
        wp_st_full = sb("wp_st", [128, 4, 64], F32)
        wp_st = wp_st_full[0:64]
        wp_bf_full = sb("wp_bf", [128, 4, 64], BF16)
        wp_bf = wp_bf_full[0:64]
        c_sb = sb("c_sb", [128, NT, 8], F32)
        negc = sb("negc", [128, NT, 8], F32)
        wstage = [sb(f"wstage{i}", [128, 898], F32) for i in range(2)]
        wcv = [sb(f"wcv{i}", [128, 898], BF16) for i in range(2)]
        big = [sb(f"big{i}", [128, 4096], BF16) for i in range(2)]
        xt = [sb(f"xt{i}", [128, D], F32) for i in range(2)]
        hn = [sb(f"hn{i}", [128, D], BF16) for i in range(2)]
        hT = sb("hT", [128, 8, 512], BF16)
        wff = sb("wff", [128, 8, 8], BF16)
        ss = sb("ss", [128, 4], F32)
        epsb = sb("epsb", [128, 1], F32)
        rstd = sb("rstd", [128, 4], F32)
        junk = sb("junk", [128, D], BF16)
        sqtmp = sb("sqtmp", [128, 512], F32)
        ktmp = sb("ktmp", [128, 512], F32)
        ssq = sb("ssq", [128, 16], F32)
        rsq = sb("rsq", [128, 16], F32)
        qa = sb("qa", [128, 8, 68], BF16)
        ka = sb("ka", [128, 8, 64], BF16)
        sqk = sb("sqk", [128, 512], BF16)
        lf_t = sb("lf_t", [128, 6, 8], F32)
        lf = sb("lf", [128, 8], F32)
        csp = sb("csp", [128, 6, 8], F32)
        csb = sb("csb", [128, NT, 3, 8], BF16)
        lfs = sb("lfs", [128, 3, 8], BF16)
        QTst = sb("QTst", [128, 8, 256], BF16)
        KTst = sb("KTst", [128, 8, 256], BF16)
        QsTst = sb("QsTst", [128, 2, 256], BF16)
        KsTst = sb("KsTst", [128, 2, 256], BF16)
        Vst = sb("Vst", [128, 8, 4, 65], BF16)
        sVst = sb("sVst", [128, 4, 4, 64], BF16)
        pxt = sb("pxt", [128, 5, 256], BF16)
        pooledT_full = sb("pooledT", [128, 4, 128], BF16)
        pooledT = pooledT_full[0:64]
        ytmp = sb("ytmp", [128, 256], BF16)
        o2 = sb("o2", [128, 260], F32)
        o3 = sb("o3", [128, 256], BF16)
        Vh = [sb(f"Vh{i}", [128, NT, 65], BF16) for i in range(2)]
        QTb = [sb(f"QTb{i}", [128, 512], BF16) for i in range(2)]
        Pt = [sb(f"Pt{i}", [128, 512], BF16) for i in range(4)]
        rec = sb("rec", [128, 4], F32)
        osb = [sb(f"osb{i}", [128, 512], F32) for i in range(2)]
        e_sb = [sb(f"e_sb{i}", [128, 512], F32) for i in range(2)]
        sp_sb = [sb(f"sp_sb{i}", [128, 512], BF16) for i in range(2)]
        arg_sb = [sb(f"arg_sb{i}", [128, 512], F32) for i in range(2)]
        carry = sb("carry", [128, 512], F32)
        xo = [sb(f"xo{i}", [128, D], F32) for i in range(2)]
        PS = [st.enter_context(nc.psum_tensor(f"ps{i}", [128, 512], F32)) for i in range(8)]

        B = {}

        def bf(name):
            if name not in B:
                B[name] = Buf(name)
            return B[name]

        vfB = {j: [bf(f"vfx{j}")] for j in range(NBLK)}
        vsB = {j: [bf(f"vsx{j}")] for j in range(NBLK)}
        ident = cb[:, CB_ID:CB_ID + 128]
        Lmat = cb[:, CB_L:CB_L + 128]
        Esel = cb[:, CB_E:CB_E + 128]
        identf = cf[:, CF_ID:CF_ID + 128]

        sc.dma("sp", cb[:, :], cb_d[:, :], writes=[bf("cb")])
        sc.dma("sp", cf[:, :], cf_d[:, :], writes=[bf("cf")])
        sc.dma("sp", gcol[:, :], gcol_d[:, :], writes=[bf("gcol")])
        sc.dma("sp", gq[:, :], gq_d[:, :], writes=[bf("gq")])
        sc.dma("sp", gk[:, :], gk_d[:, :], writes=[bf("gk")])
        sc.dma("sp", bfb[:, :], bf_d[:, :], writes=[bf("bfb")])
        sc.op("pool", lambda h: h.memset(Vst[:, :, :, 64:65], 1.0), writes=[bf("Vst")])
        sc.op("pool", lambda h: h.memset(KTst[64:67, :, :], 1.0), writes=[bf("KTst")])
        sc.op("pool", lambda h: h.memset(qa[:, :, 67:68], 0.0), writes=[bf("qa")])
        sc.op("pool", lambda h: h.memset(epsb[:, :], EPS), writes=[bf("epsb")])

        wq = ["sp", "pool"]

        wbi_bufs = {l: [] for l in range(DEPTH)}
        wbo_bufs = {l: [] for l in range(DEPTH)}

        def convert_weights(l):
            k = 0
            for kt in range(8):
                for pc in range(4):
                    c0 = pc * 898
                    stg, sB = wstage[k % 2], bf(f"wstage{k % 2}")
                    cv, cB = wcv[k % 2], bf(f"wcv{k % 2}")
                    sc.dma("sp", stg[:, 0:898], w_in[l, kt * 128:(kt + 1) * 128, c0:c0 + 898],
                           writes=[sB])
                    sc.op("pool", lambda h, stg=stg, cv=cv, kt=kt, l=l: h.tensor_scalar(
                        out=cv[:, 0:898], in0=stg[:, 0:898],
                        scalar1=gcol[:, l * 8 + kt:l * 8 + kt + 1], scalar2=None, op0=ALU.mult),
                        reads=[sB, bf("gcol")], writes=[cB])
                    wb = Buf(f"wbi{l}_{k}")
                    wbi_bufs[l].append(wb)
                    sc.dma("pool", wbi_d[l, :, kt, c0:c0 + 898], cv[:, 0:898], reads=[cB], writes=[wb])
                    k += 1
            for kt in range(8):
                for pc in range(2):
                    c0 = pc * 512
                    stg, sB = wstage[k % 2], bf(f"wstage{k % 2}")
                    cv, cB = wcv[k % 2], bf(f"wcv{k % 2}")
                    sc.dma("sp", stg[:, 0:512], w_out[l, kt * 128:(kt + 1) * 128, c0:c0 + 512],
                           writes=[sB])
                    sc.op("pool", lambda h, stg=stg, cv=cv: h.tensor_copy(
                        out=cv[:, 0:512], in_=stg[:, 0:512]), reads=[sB], writes=[cB])
                    wb = Buf(f"wbo{l}_{k}")
                    wbo_bufs[l].append(wb)
                    sc.dma("pool", wbo_d[l, :, kt, c0:c0 + 512], cv[:, 0:512], reads=[cB], writes=[wb])
                    k += 1

        def layer_params(l):
            sc.dma("sp", psb[:, :], ps_d[:, l * 256:(l + 1) * 256], writes=[bf("psb")])
            sc.dma("sp", wp_st[:, :, :], w_pool[l].rearrange("g c d -> c g d"), writes=[bf("wp_st")])
            sc.op("pool", lambda h: h.tensor_tensor(
                out=wp_bf[:, :, :], in0=wp_st[:, :, :],
                in1=psb[0:64, :].rearrange("p (g d) -> p g d", g=4), op=ALU.mult),
                reads=[bf("wp_st"), bf("psb")], writes=[bf("wp_bf")])
            sc.op("pool", lambda h, l=l: h.tensor_tensor(
                out=gqk[:, :], in0=gq[:, l * 64:(l + 1) * 64], in1=gk[:, l * 64:(l + 1) * 64],
                op=ALU.mult), reads=[bf("gq"), bf("gk")], writes=[bf("gqk")])
            sc.op("pool", lambda h: h.tensor_scalar(out=gqk[:, :], in0=gqk[:, :], scalar1=0.125,
                                                    scalar2=None, op0=ALU.mult),
                  reads=[bf("gqk")], writes=[bf("gqk")])
            for hd in range(8):
                sc.op("pool", lambda h, hd=hd: h.tensor_copy(out=gqk8[:, hd * 64:(hd + 1) * 64], in_=gqk[:, :]),
                      reads=[bf("gqk")], writes=[bf("gqk8")])

        psn = {"a": 0, "o": 0, "c": 0}
        pools = {"a": [0, 1, 2, 3, 4, 5], "o": [6, 7], "z": [0, 1, 2, 3], "c": [4, 5]}

        def psum(kind="a"):
            key = "a" if kind == "z" else kind
            lst = pools[kind]
            i = lst[psn[key] % len(lst)]
            psn[key] += 1
            return PS[i], bf(f"ps{i}")

        def phase_a(l, x_src):
            xv = x_src.rearrange("(i p) d -> i p d", p=128)
            for j in range(DEBUG.get("nblk", NBLK)):
                for s in range(4):
                    i = 4 * j + s
                    xb, xB = xt[i % 2], bf(f"xt{i % 2}")
                    hb, hB = hn[i % 2], bf(f"hn{i % 2}")
                    sc.dma("sp", xb[:, :], xv[i], reads=[bf(f"x{l}_{i}")] if l else [], writes=[xB])
                    sc.op("act", lambda h, xb=xb, s=s: h.activation(
                        out=junk[:, :], in_=xb[:, :], func=AF.Square, accum_out=ss[:, s:s + 1]),
                        reads=[xB], writes=[bf("junk"), bf("ss")])
                    sc.op("act", lambda h, s=s: h.activation(
                        out=rstd[:, s:s + 1], in_=ss[:, s:s + 1], func=AF.Ln, scale=1.0 / D, bias=epsb[:, 0:1]),
                        reads=[bf("ss"), bf("epsb")], writes=[bf("rstd")])
                    sc.op("act", lambda h, s=s: h.activation(
                        out=rstd[:, s:s + 1], in_=rstd[:, s:s + 1], func=AF.Exp, scale=-0.5),
                        reads=[bf("rstd")], writes=[bf("rstd")])
                    sc.op("dve", lambda h, xb=xb, hb=hb, s=s: h.tensor_scalar(
                        out=hb[:, :], in0=xb[:, :], scalar1=rstd[:, s:s + 1], scalar2=None,
                        op0=ALU.mult), reads=[xB, bf("rstd")], writes=[hB])
                    pt, pB = psum()
                    ptb = pt[:, :].bitcast(BF16)
                    for kt in range(8):
                        sc.op("pe", lambda h, ptb=ptb, hb=hb, kt=kt: h.transpose(
                            out=ptb[:, kt * 128:(kt + 1) * 128], in_=hb[:, kt * 128:(kt + 1) * 128],
                            identity=ident), reads=[hB, bf("cb")], writes=[pB], inc=(kt == 7))
                    sc.op("act", lambda h, ptb=ptb, s=s: h.activation(
                        out=hT[:, :, s * 128:(s + 1) * 128],
                        in_=ptb.rearrange("p (k t) -> p k t", k=8), func=AF.Copy),
                        reads=[pB], writes=[bf("hT")])

                sc.dma("sp", wff[:, :, :], wbi_d[l, :, :, C_FF:C_FF + 8], reads=wbi_bufs[l],
                       writes=[bf("wff")])
                for s in range(4):
                    i = 4 * j + s
                    pt, pB = psum()
                    for kt in range(8):
                        sc.op("pe", lambda h, pt=pt, kt=kt, s=s: h.matmul(
                            pt[:, 0:8], lhsT=hT[:, kt, s * 128:(s + 1) * 128], rhs=wff[:, kt, :],
                            start=(kt == 0), stop=(kt == 7)),
                            reads=[bf("hT"), bf("wff")], writes=[pB], inc=(kt == 7))
                    v, av, ex, ln_, mn = (lf_t[:, k, :] for k in range(5))
                    sc.op("dve", lambda h, pt=pt, v=v, l=l: h.tensor_tensor(
                        out=v, in0=pt[:, 0:8], in1=bfb[:, l * 8:(l + 1) * 8], op=ALU.add),
                        reads=[pB, bf("bfb")], writes=[bf("lf_t")])
                    sc.op("act", lambda h, v=v, av=av: h.activation(out=av, in_=v, func=AF.Abs),
                          reads=[bf("lf_t")], writes=[bf("lf_t")])
                    sc.op("act", lambda h, av=av, ex=ex: h.activation(
                        out=ex, in_=av, func=AF.Exp, scale=-1.0),
                        reads=[bf("lf_t")], writes=[bf("lf_t")])
                    sc.op("act", lambda h, ex=ex, ln_=ln_: h.activation(
                        out=ln_, in_=ex, func=AF.Ln, bias=1.0),
                        reads=[bf("lf_t")], writes=[bf("lf_t")])
                    sc.op("dve", lambda h, v=v, mn=mn: h.tensor_scalar(
                        out=mn, in0=v, scalar1=0.0, scalar2=None, op0=ALU.min),
                        reads=[bf("lf_t")], writes=[bf("lf_t")])
                    sc.op("dve", lambda h, mn=mn, ln_=ln_: h.tensor_tensor(
                        out=lf[:, :], in0=mn, in1=ln_, op=ALU.subtract),
                        reads=[bf("lf_t")], writes=[bf("lf")])
                    r1, r2 = csp[:, 4, :], csp[:, 5, :]
                    sc.op("dve", lambda h: h.tensor_copy(out=lfs[:, 0, :], in_=lf[:, :]),
                          reads=[bf("lf")], writes=[bf("lfs")])
                    sc.op("dve", lambda h, r1=r1: h.tensor_tensor(
                        out=r1, in0=lf[:, :], in1=lfs[:, 0, :], op=ALU.subtract),
                        reads=[bf("lf"), bf("lfs")], writes=[bf("csp")])
                    sc.op("dve", lambda h, r1=r1: h.tensor_copy(out=lfs[:, 1, :], in_=r1),
                          reads=[bf("csp")], writes=[bf("lfs")])
                    sc.op("dve", lambda h, r1=r1, r2=r2: h.tensor_tensor(
                        out=r2, in0=r1, in1=lfs[:, 1, :], op=ALU.subtract),
                        reads=[bf("csp"), bf("lfs")], writes=[bf("csp")])
                    sc.op("dve", lambda h, r2=r2: h.tensor_copy(out=lfs[:, 2, :], in_=r2),
                          reads=[bf("csp")], writes=[bf("lfs")])
                    pc, pcB = psum()
                    nmm = 3 if i == 0 else 6
                    for k in range(3):
                        sc.op("pe", lambda h, pc=pc, k=k, nmm=nmm: h.matmul(
                            pc[:, 0:8], lhsT=Lmat, rhs=lfs[:, k, :], start=(k == 0), stop=(k == nmm - 1)),
                            reads=[bf("lfs"), bf("cb")], writes=[pcB], inc=(k == nmm - 1))
                    if i > 0:
                        for k in range(3):
                            sc.op("pe", lambda h, pc=pc, i=i, k=k: h.matmul(
                                pc[:, 0:8], lhsT=Esel, rhs=csb[:, i - 1, k, :], start=False, stop=(k == 2)),
                                reads=[bf("csb"), bf("cb")], writes=[pcB], inc=(k == 2))
                    sc.op("dve", lambda h, pc=pc, i=i: h.tensor_copy(out=c_sb[:, i, :], in_=pc[:, 0:8]),
                          reads=[pcB], writes=[bf("c_sb")])
                    sc.op("dve", lambda h, i=i: h.tensor_scalar(
                        out=negc[:, i, :], in0=c_sb[:, i, :], scalar1=-1.0, scalar2=None,
                        op0=ALU.mult), reads=[bf("c_sb")], writes=[bf("negc")])
                    t1, t2 = csp[:, 0, :], csp[:, 1, :]
                    sc.op("dve", lambda h, i=i: h.tensor_copy(out=csb[:, i, 0, :], in_=c_sb[:, i, :]),
                          reads=[bf("c_sb")], writes=[bf("csb")])
                    sc.op("dve", lambda h, i=i, t1=t1: h.tensor_tensor(
                        out=t1, in0=c_sb[:, i, :], in1=csb[:, i, 0, :], op=ALU.subtract),
                        reads=[bf("c_sb"), bf("csb")], writes=[bf("csp")])
                    sc.op("dve", lambda h, i=i, t1=t1: h.tensor_copy(out=csb[:, i, 1, :], in_=t1),
                          reads=[bf("csp")], writes=[bf("csb")])
                    sc.op("dve", lambda h, i=i, t1=t1, t2=t2: h.tensor_tensor(
                        out=t2, in0=t1, in1=csb[:, i, 1, :], op=ALU.subtract),
                        reads=[bf("csp"), bf("csb")], writes=[bf("csp")])
                    sc.op("dve", lambda h, i=i, t2=t2: h.tensor_copy(out=csb[:, i, 2, :], in_=t2),
                          reads=[bf("csp")], writes=[bf("csb")])

                for ci, (kind, c0) in enumerate(chunks):
                    if kind not in DEBUG.get("parts", "q k v g pp sqk svg").split():
                        continue
                    n = j * 7 + ci
                    gi = n % 2
                    wch = big[gi][:, :].rearrange("p (k c) -> p k c", k=8)
                    wB = bf(f"big{gi}")
                    if n == 0:
                        load_chunk(l, 0)
                    if n + 1 < NBLK * 7:
                        load_chunk(l, n + 1)
                    for s in range(4):
                        i = 4 * j + s
                        pt, pB = psum()
                        for kt in range(8):
                            sc.op("pe", lambda h, pt=pt, kt=kt, s=s, wch=wch: h.matmul(
                                pt[:, :], lhsT=hT[:, kt, s * 128:(s + 1) * 128], rhs=wch[:, kt, :],
                                start=(kt == 0), stop=(kt == 7)),
                                reads=[bf("hT"), wB], writes=[pB], inc=(kt == 7))
                        evac(l, j, s, i, kind, pt, pB)

        chunks = [("q", C_FQ), ("k", C_FK), ("v", C_FV), ("g", C_FG), ("pp", C_PX),
                  ("sqk", C_SQ), ("svg", C_SV)]

        def load_chunk(l, n):
            c0 = chunks[n % 7][1]
            gi = n % 2
            wch = big[gi][:, :].rearrange("p (k c) -> p k c", k=8)
            sc.dma("sp", wch, wbi_d[l, :, :, c0:c0 + 512], reads=wbi_bufs[l], writes=[bf(f"big{gi}")])

        def qk_norm(pt, pB, col):
            sc.op("act", lambda h, pt=pt: h.activation(out=sqtmp[:, :], in_=pt[:, :], func=AF.Square),
                  reads=[pB], writes=[bf("sqtmp")])
            sc.op("dve", lambda h, col=col: h.tensor_reduce(
                out=ssq[:, col:col + 8], in_=sqtmp[:, :].rearrange("p (h d) -> p h d", h=8),
                axis=AX.X, op=ALU.add), reads=[bf("sqtmp")], writes=[bf("ssq")])
            sc.op("act", lambda h, col=col: h.activation(
                out=rsq[:, col:col + 8], in_=ssq[:, col:col + 8], func=AF.Ln, scale=1.0 / 64, bias=epsb[:, 0:1]),
                reads=[bf("ssq"), bf("epsb")], writes=[bf("rsq")])
            sc.op("act", lambda h, col=col: h.activation(
                out=rsq[:, col:col + 8], in_=rsq[:, col:col + 8], func=AF.Exp, scale=-0.5),
                reads=[bf("rsq")], writes=[bf("rsq")])

        def evac(l, j, s, i, kind, pt, pB):
            half = s // 2
            so = (s % 2) * 128
            t0 = j * 512 + half * 256
            if kind == "q":
                qk_norm(pt, pB, 0)
                for hd in range(8):
                    sc.op("act", lambda h, pt=pt, hd=hd: h.activation(
                        out=qa[:, hd, 0:64], in_=pt[:, hd * 64:(hd + 1) * 64], func=AF.Copy,
                        scale=rsq[:, hd:hd + 1]), reads=[pB, bf("rsq")], writes=[bf("qa")])
                for k3 in range(3):
                    sc.op("dve", lambda h, i=i, k3=k3: h.tensor_copy(
                        out=qa[:, :, 64 + k3], in_=csb[:, i, k3, :]),
                        reads=[bf("csb")], writes=[bf("qa")])
                pq, pqB = psum()
                pqb = pq[:, :].bitcast(BF16)
                for hd in range(8):
                    sc.op("pe", lambda h, pqb=pqb, hd=hd: h.transpose(
                        out=pqb[0:67, hd * 128:(hd + 1) * 128], in_=qa[:, hd, 0:67], identity=ident),
                        reads=[bf("qa"), bf("cb")], writes=[pqB], inc=(hd == 7))
                sc.op("act", lambda h, pqb=pqb, so=so: h.activation(
                    out=QTst[0:67, :, so:so + 128],
                    in_=pqb[0:67, :].rearrange("p (k t) -> p k t", k=8), func=AF.Copy),
                    reads=[pqB], writes=[bf("QTst")])
                if s % 2 == 1:
                    sc.dma("sp", qt_d[:, :, t0:t0 + 256].rearrange("h r t -> r h t"),
                           QTst[0:67, :, :], reads=[bf("QTst")], writes=[bf(f"qt{j}")])
            elif kind == "k":
                qk_norm(pt, pB, 8)
                for hd in range(8):
                    sc.op("act", lambda h, pt=pt, hd=hd: h.activation(
                        out=ktmp[:, hd * 64:(hd + 1) * 64], in_=pt[:, hd * 64:(hd + 1) * 64],
                        func=AF.Copy, scale=rsq[:, 8 + hd:9 + hd]),
                        reads=[pB, bf("rsq")], writes=[bf("ktmp")])
                sc.op("dve", lambda h: h.tensor_tensor(
                    out=ka[:, :, :].rearrange("p h d -> p (h d)"), in0=ktmp[:, :], in1=gqk8[:, :],
                    op=ALU.mult), reads=[bf("ktmp"), bf("gqk8")], writes=[bf("ka")])
                pq, pqB = psum()
                pqb = pq[:, :].bitcast(BF16)
                for hd in range(8):
                    sc.op("pe", lambda h, pqb=pqb, hd=hd: h.transpose(
                        out=pqb[0:64, hd * 128:(hd + 1) * 128], in_=ka[:, hd, :], identity=ident),
                        reads=[bf("ka"), bf("cb")], writes=[pqB], inc=(hd == 7))
                sc.op("act", lambda h, pqb=pqb, so=so: h.activation(
                    out=KTst[0:64, :, so:so + 128],
                    in_=pqb[0:64, :].rearrange("p (k t) -> p k t", k=8), func=AF.Copy),
                    reads=[pqB], writes=[bf("KTst")])
                if s % 2 == 1:
                    sc.dma("sp", kt_d[:, :, t0:t0 + 256].rearrange("h r t -> r h t"),
                           KTst[0:67, :, :], reads=[bf("KTst")], writes=[bf(f"kt{j}")])
            elif kind == "v":
                sc.op("act", lambda h, pt=pt, s=s: h.activation(
                    out=Vst[:, :, s, 0:64], in_=pt[:, :].rearrange("p (h d) -> p h d", h=8),
                    func=AF.Copy), reads=[pB], writes=[bf("Vst")])
                if s == 3:
                    vfB[j] = []
                    for hh in range(8):
                        wb = Buf(f"vf{j}_{hh}")
                        vfB[j].append(wb)
                        sc.dma("sp", vf_d[:, hh, 4 * j:4 * j + 4, :].rearrange("p a b -> p (a b)"),
                               Vst[:, hh, :, :].rearrange("p a b -> p (a b)"),
                               reads=[bf("Vst")], writes=[wb])
            elif kind == "g":
                sc.op("act", lambda h, pt=pt, i=i: h.activation(
                    out=GM[:, i, 0:512], in_=pt[:, :], func=AF.Silu),
                    reads=[pB], writes=[bf(f"GM{i}")])
            elif kind == "pp":
                slot = 1 + s
                if s == 0 and j > 0:
                    sc.op("dve", lambda h: h.tensor_copy(out=pxt[:, 0, :], in_=pxt[:, 4, :]),
                          reads=[bf("pxt")], writes=[bf("pxt")])
                sc.op("act", lambda h, pt=pt, slot=slot: h.activation(
                    out=pxt[:, slot, :], in_=pt[:, 0:256], func=AF.Copy), reads=[pB], writes=[bf("pxt")])
                sc.op("act", lambda h, pt=pt, i=i: h.activation(
                    out=GM[:, i, 512:768], in_=pt[:, 256:512], func=AF.Silu),
                    reads=[pB], writes=[bf(f"GM{i}")])
                pp, ppB = psum()
                for g in range(4):
                    base = CB_POOL + g * 3 * 128
                    m_same = cb[:, base + (256 if i == 0 else 0):base + (256 if i == 0 else 0) + 128]
                    m_prev = cb[:, base + 128:base + 256]
                    sc.op("pe", lambda h, pp=pp, g=g, slot=slot, m_same=m_same, i=i: h.matmul(
                        pp[0:64, g * 128:(g + 1) * 128], lhsT=pxt[:, slot, g * 64:(g + 1) * 64],
                        rhs=m_same, start=True, stop=(i == 0)),
                        reads=[bf("pxt"), bf("cb")], writes=[ppB], inc=(i == 0 and g == 3))
                    if i > 0:
                        sc.op("pe", lambda h, pp=pp, g=g, slot=slot, m_prev=m_prev: h.matmul(
                            pp[0:64, g * 128:(g + 1) * 128], lhsT=pxt[:, slot - 1, g * 64:(g + 1) * 64],
                            rhs=m_prev, start=False, stop=True),
                            reads=[bf("pxt"), bf("cb")], writes=[ppB], inc=(g == 3))
                sc.op("act", lambda h, pp=pp: h.activation(
                    out=pooledT[:, :, :].rearrange("p g t -> p (g t)"), in_=pp[0:64, :], func=AF.Copy),
                    reads=[ppB], writes=[bf("pooledT")])
                py, pyB = psum()
                for g in range(4):
                    sc.op("pe", lambda h, py=py, g=g: h.matmul(
                        py[:, g * 64:(g + 1) * 64], lhsT=pooledT[:, g, :], rhs=wp_bf[:, g, :],
                        start=True, stop=True),
                        reads=[bf("pooledT"), bf("wp_bf")], writes=[pyB], inc=(g == 3))
                sc.op("act", lambda h, py=py: h.activation(out=ytmp[:, :], in_=py[:, 0:256], func=AF.Copy),
                      reads=[pyB], writes=[bf("ytmp")])
                sc.op("dve", lambda h, i=i: h.tensor_tensor(
                    out=GM[:, i, 512:768], in0=ytmp[:, :], in1=GM[:, i, 512:768], op=ALU.mult),
                    reads=[bf("ytmp"), bf(f"GM{i}")], writes=[bf(f"GM{i}")])
            elif kind == "sqk":
                sc.op("act", lambda h, pt=pt: h.activation(
                    out=sqk[:, 0:256], in_=pt[:, 0:256], func=AF.Copy, scale=0.125),
                    reads=[pB], writes=[bf("sqk")])
                sc.op("act", lambda h, pt=pt: h.activation(out=sqk[:, 256:512], in_=pt[:, 256:512], func=AF.Copy),
                      reads=[pB], writes=[bf("sqk")])
                pq, pqB = psum()
                pqb = pq[:, :].bitcast(BF16)
                for k in range(4):
                    sc.op("pe", lambda h, pqb=pqb, k=k: h.transpose(
                        out=pqb[:, k * 128:(k + 1) * 128], in_=sqk[:, k * 128:(k + 1) * 128],
                        identity=ident), reads=[bf("sqk"), bf("cb")], writes=[pqB], inc=(k == 3))
                sc.op("act", lambda h, pqb=pqb, so=so: h.activation(
                    out=QsTst[:, :, so:so + 128],
                    in_=pqb[:, 0:256].rearrange("p (k t) -> p k t", k=2), func=AF.Copy),
                    reads=[pqB], writes=[bf("QsTst")])
                sc.op("act", lambda h, pqb=pqb, so=so: h.activation(
                    out=KsTst[:, :, so:so + 128],
                    in_=pqb[:, 256:512].rearrange("p (k t) -> p k t", k=2), func=AF.Copy),
                    reads=[pqB], writes=[bf("KsTst")])
                if s % 2 == 1:
                    sc.dma("sp", qs_d[:, :, t0:t0 + 256].rearrange("(a b) r t -> (b r) a t", b=2),
                           QsTst[:, :, :], reads=[bf("QsTst")], writes=[bf(f"qs{j}")])
                    sc.dma("sp", ks_d[:, :, t0:t0 + 256].rearrange("(a b) r t -> (b r) a t", b=2),
                           KsTst[:, :, :], reads=[bf("KsTst")], writes=[bf(f"ks{j}")])
            elif kind == "svg":
                sc.op("act", lambda h, pt=pt, s=s: h.activation(
                    out=sVst[:, :, s, :], in_=pt[:, 0:256].rearrange("p (h d) -> p h d", h=4),
                    func=AF.Copy), reads=[pB], writes=[bf("sVst")])
                sc.op("act", lambda h, pt=pt, i=i: h.activation(
                    out=GM[:, i, 768:1024], in_=pt[:, 256:512], func=AF.Silu),
                    reads=[pB], writes=[bf(f"GM{i}")])
                if s == 3:
                    vsB[j] = []
                    for hh in range(4):
                        wb = Buf(f"vs{j}_{hh}")
                        vsB[j].append(wb)
                        sc.dma("sp", vs_d[:, hh, 4 * j:4 * j + 4, :].rearrange("p a b -> p (a b)"),
                               sVst[:, hh, :, :].rearrange("p a b -> p (a b)"),
                               reads=[bf("sVst")], writes=[wb])

        def load_kv(kind, hd):
            KT, KB = big[hd % 2], bf(f"big{hd % 2}")
            Vb, VB = Vh[hd % 2], bf(f"Vh{hd % 2}")
            if kind == "f":
                sc.dma("sp", KT[0:67, :], kt_d[hd, :, :], reads=[bf(f"kt{j}") for j in range(NBLK)],
                       writes=[KB])
                sc.dma("sp", Vb[:, :, :].rearrange("p a b -> p (a b)"),
                       vf_d[:, hd, :, :].rearrange("p a b -> p (a b)"),
                       reads=[w for j in range(NBLK) for w in vfB[j]], writes=[VB])
            else:
                sc.dma("sp", KT[0:64, :], ks_d[hd, :, :], reads=[bf(f"ks{j}") for j in range(NBLK)],
                       writes=[KB])
                sc.dma("sp", Vb[:, :, 0:64], vs_d[:, hd, :, :],
                       reads=[w for j in range(NBLK) for w in vsB[j]], writes=[VB])

        def load_q(kind, hd, qb, slot):
            Qb, QB = QTb[slot], bf(f"QTb{slot}")
            if kind == "f":
                sc.dma("sp", Qb[0:67, :], qt_d[hd, :, qb * 512:(qb + 1) * 512],
                       reads=[bf(f"qt{qb}")], writes=[QB])
            else:
                sc.dma("sp", Qb[0:64, :], qs_d[hd, :, qb * 512:(qb + 1) * 512],
                       reads=[bf(f"qs{qb}")], writes=[QB])

        def phase_c_fox(l):
            load_kv("f", 0)
            load_q("f", 0, 0, 0)
            qn = 0
            for hd in range(8):
                KT, KB = big[hd % 2], bf(f"big{hd % 2}")
                Vb, VB = Vh[hd % 2], bf(f"Vh{hd % 2}")
                for qb in range(NBLK):
                    Qb, QB = QTb[qn % 2], bf(f"QTb{qn % 2}")
                    if qb + 1 < NBLK:
                        load_q("f", hd, qb + 1, (qn + 1) % 2)
                    elif hd + 1 < 8:
                        load_kv("f", hd + 1)
                        load_q("f", hd + 1, 0, (qn + 1) % 2)
                    qn += 1
                    po, poB = psum("o")
                    pov = po[:, 0:260].rearrange("p (s d) -> p s d", s=4)
                    nk = 4 * (qb + 1)
                    st_ = {}

                    def stage_a(ki):
                        o = ki - 4 * qb
                        pss, psB = psum("a")
                        sc.op("pe", lambda h, pss=pss, ki=ki, o=o, KT=KT, Qb=Qb: h.matmul(
                            pss[:, :], lhsT=KT[0:67, ki * 128:(ki + 1) * 128], rhs=Qb[0:67, :],
                            start=True, stop=(o < 0)), reads=[KB, QB], writes=[psB], inc=(o < 0))
                        if o >= 0:
                            mb = cb[:, CB_MBF + o * 512:CB_MBF + (o + 1) * 512]
                            sc.op("pe", lambda h, pss=pss, mb=mb: h.matmul(
                                pss[:, :], lhsT=ident, rhs=mb, start=False, stop=True),
                                reads=[bf("cb")], writes=[psB])
                        P, PB = Pt[ki % 4], bf(f"Pt{ki % 4}")
                        sc.op("act", lambda h, pss=pss, P=P, ki=ki, hd=hd: h.activation(
                            out=P[:, :], in_=pss[:, :], func=AF.Exp, bias=negc[:, ki, hd:hd + 1]),
                            reads=[psB, bf("negc")], writes=[PB])
                        st_[ki] = (P, PB)

                    def stage_c(ki):
                        P, PB = st_.pop(ki)
                        sc.op("pe", lambda h, P=P, ki=ki, po=po, Vb=Vb, nk=nk: h.matmul(
                            po[0:65, :], lhsT=Vb[:, ki, :], rhs=P[:, :],
                            start=(ki == 0), stop=(ki == nk - 1)),
                            reads=[PB, VB], writes=[poB], inc=True)

                    for k in range(nk + 2):
                        if k < nk:
                            stage_a(k)
                        if k >= 2:
                            stage_c(k - 2)
                    ob, oB = osb[qn % 2], bf(f"osb{qn % 2}")
                    sc.op("act", lambda h, po=po, ob=ob: h.activation(
                        out=ob[0:65, :], in_=po[0:65, :], func=AF.Copy), reads=[poB], writes=[oB])
                    p2, p2B = psum("a")
                    p2v = p2[:, 0:260].rearrange("p (s d) -> p s d", s=4)
                    for s4 in range(4):
                        sc.op("pe", lambda h, p2v=p2v, ob=ob, s4=s4: h.transpose(
                            out=p2v[:, s4, :], in_=ob[0:65, s4 * 128:(s4 + 1) * 128],
                            identity=identf[0:65, 0:65]), reads=[oB, bf("cf")], writes=[p2B],
                            inc=(s4 == 3))
                    sc.op("act", lambda h, p2=p2: h.activation(out=o2[:, :], in_=p2[:, 0:260], func=AF.Copy),
                          reads=[p2B], writes=[bf("o2")])
                    o2v = o2[:, :].rearrange("p (s d) -> p s d", s=4)
                    sc.op("dve", lambda h, o2v=o2v: h.reciprocal(out=rec[:, :], in_=o2v[:, :, 64]),
                          reads=[bf("o2")], writes=[bf("rec")])
                    for s4 in range(4):
                        i = 4 * qb + s4
                        sc.op("dve", lambda h, o2v=o2v, s4=s4: h.tensor_scalar(
                            out=o3[:, s4 * 64:(s4 + 1) * 64], in0=o2v[:, s4, 0:64],
                            scalar1=rec[:, s4:s4 + 1], scalar2=None, op0=ALU.mult),
                            reads=[bf("o2"), bf("rec")], writes=[bf("o3")])
                        sc.op("dve", lambda h, s4=s4, i=i, hd=hd: h.tensor_tensor(
                            out=GM[:, i, hd * 64:(hd + 1) * 64], in0=o3[:, s4 * 64:(s4 + 1) * 64],
                            in1=GM[:, i, hd * 64:(hd + 1) * 64], op=ALU.mult),
                            reads=[bf("o3"), bf(f"GM{i}")], writes=[bf(f"GM{i}")])

        def phase_c_sb(l):
            uneg = cb[:, CB_UNEG:CB_UNEG + 128]
            oneg = cb[:, CB_ONEG:CB_ONEG + 128]
            load_kv("s", 0)
            load_q("s", 0, 0, 0)
            qn = 0
            for hd in range(4):
                KT, KB = big[hd % 2], bf(f"big{hd % 2}")
                Vb, VB = Vh[hd % 2], bf(f"Vh{hd % 2}")
                Vv = Vb[:, :, 0:64]
                for qb in range(NBLK):
                    Qb, QB = QTb[qn % 2], bf(f"QTb{qn % 2}")
                    if qb + 1 < NBLK:
                        load_q("s", hd, qb + 1, (qn + 1) % 2)
                    elif hd + 1 < 4:
                        load_kv("s", hd + 1)
                        load_q("s", hd + 1, 0, (qn + 1) % 2)
                    qn += 1
                    po, poB = psum("o")
                    pov = po[:, 0:256].rearrange("p (s d) -> p s d", s=4)
                    nk = 4 * (qb + 1)
                    st_ = {}

                    def stage_a(n):
                        ki = nk - 1 - n
                        o = ki - 4 * qb
                        pz, pzB = psum("z")
                        sc.op("pe", lambda h, pz=pz, ki=ki, o=o, KT=KT, Qb=Qb: h.matmul(
                            pz[:, :], lhsT=KT[0:64, ki * 128:(ki + 1) * 128], rhs=Qb[0:64, :],
                            start=True, stop=(o < 0)), reads=[KB, QB], writes=[pzB], inc=(o < 0))
                        if o >= 0:
                            mb = cb[:, CB_MBS + o * 512:CB_MBS + (o + 1) * 512]
                            sc.op("pe", lambda h, pz=pz, mb=mb: h.matmul(
                                pz[:, :], lhsT=ident, rhs=mb, start=False, stop=True),
                                reads=[bf("cb")], writes=[pzB])
                        eb, eB = e_sb[n % 2], bf(f"e_sb{n % 2}")
                        spb, spB = sp_sb[n % 2], bf(f"sp_sb{n % 2}")
                        sc.op("act", lambda h, pz=pz, eb=eb: h.activation(
                            out=eb[:, :], in_=pz[:, :], func=AF.Exp), reads=[pzB], writes=[eB])
                        sc.op("act", lambda h, eb=eb, spb=spb: h.activation(
                            out=spb[:, :], in_=eb[:, :], func=AF.Ln, bias=1.0),
                            reads=[eB], writes=[spB])
                        st_[n] = dict(ki=ki, pz=pz, pzB=pzB, spb=spb, spB=spB)

                    def stage_b(n):
                        d = st_[n]
                        pz, pzB, spb, spB, ki = d["pz"], d["pzB"], d["spb"], d["spB"], d["ki"]
                        sc.op("pe", lambda h, pz=pz, spb=spb: h.matmul(
                            pz[:, :], lhsT=uneg, rhs=spb[:, :], start=False, stop=True,
                            skip_group_check=True),
                            reads=[spB, bf("cb")], writes=[pzB])
                        pcs, pcsB = psum("c")
                        sc.op("pe", lambda h, pcs=pcs, spb=spb: h.matmul(
                            pcs[:, :], lhsT=oneg, rhs=spb[:, :], start=True, stop=True),
                            reads=[spB, bf("cb")], writes=[pcsB])
                        ab, aB = arg_sb[n % 2], bf(f"arg_sb{n % 2}")
                        P, PB = Pt[n % 3], bf(f"Pt{n % 3}")
                        if n == 0:
                            sc.op("act", lambda h, pz=pz, P=P: h.activation(
                                out=P[:, :], in_=pz[:, :], func=AF.Exp), reads=[pzB], writes=[PB])
                            sc.op("dve", lambda h, pcs=pcs: h.tensor_copy(out=carry[:, :], in_=pcs[:, :]),
                                  reads=[pcsB], writes=[bf("carry")])
                        else:
                            sc.op("dve", lambda h, pz=pz, ab=ab: h.tensor_tensor(
                                out=ab[:, :], in0=pz[:, :], in1=carry[:, :], op=ALU.add),
                                reads=[pzB, bf("carry")], writes=[aB])
                            sc.op("act", lambda h, ab=ab, P=P: h.activation(
                                out=P[:, :], in_=ab[:, :], func=AF.Exp), reads=[aB], writes=[PB])
                            if ki > 0:
                                sc.op("dve", lambda h, pcs=pcs: h.tensor_tensor(
                                    out=carry[:, :], in0=pcs[:, :], in1=carry[:, :], op=ALU.add),
                                    reads=[pcsB, bf("carry")], writes=[bf("carry")])
                        d["P"], d["PB"] = P, PB

                    def stage_c(n):
                        d = st_.pop(n)
                        P, PB, ki = d["P"], d["PB"], d["ki"]
                        sc.op("pe", lambda h, P=P, ki=ki, n=n, po=po, Vv=Vv, nk=nk: h.matmul(
                            po[0:64, :], lhsT=Vv[:, ki, :], rhs=P[:, :],
                            start=(n == 0), stop=(n == nk - 1)),
                            reads=[PB, VB], writes=[poB], inc=True)

                    for k in range(nk + 2):
                        if k < nk:
                            stage_a(k)
                        if 1 <= k <= nk:
                            stage_b(k - 1)
                        if k >= 2:
                            stage_c(k - 2)
                    ob, oB = osb[qn % 2], bf(f"osb{qn % 2}")
                    sc.op("dve", lambda h, po=po, ob=ob: h.tensor_copy(out=ob[0:64, :], in_=po[0:64, :]),
                          reads=[poB], writes=[oB])
                    p2, p2B = psum("c")
                    p2v = p2[:, 0:256].rearrange("p (s d) -> p s d", s=4)
                    for s4 in range(4):
                        sc.op("pe", lambda h, p2v=p2v, ob=ob, s4=s4: h.transpose(
                            out=p2v[:, s4, :], in_=ob[0:64, s4 * 128:(s4 + 1) * 128],
                            identity=identf[0:64, 0:64]), reads=[oB, bf("cf")], writes=[p2B],
                            inc=(s4 == 3))
                    c0 = 768 + hd * 64
                    sc.op("act", lambda h, p2=p2: h.activation(out=o3[:, :], in_=p2[:, 0:256], func=AF.Copy),
                          reads=[p2B], writes=[bf("o3")])
                    for s4 in range(4):
                        i = 4 * qb + s4
                        sc.op("dve", lambda h, s4=s4, i=i, c0=c0: h.tensor_tensor(
                            out=GM[:, i, c0:c0 + 64], in0=o3[:, s4 * 64:(s4 + 1) * 64],
                            in1=GM[:, i, c0:c0 + 64], op=ALU.mult),
                            reads=[bf("o3"), bf(f"GM{i}")], writes=[bf(f"GM{i}")])

        def phase_d(l, x_src, x_dst, last):
            xv = x_src.rearrange("(i p) d -> i p d", p=128)
            ov = x_dst.rearrange("(i p) d -> i p d", p=128)
            wo = [big[k][:, :].rearrange("p (k c) -> p k c", k=8) for k in range(2)]
            for k in range(2):
                sc.dma("sp", wo[k], wbo_d[l, :, :, k * 512:(k + 1) * 512], reads=wbo_bufs[l],
                       writes=[bf(f"big{k}")])
            toks = []
            for i in range(NT):
                xb, xB = xt[i % 2], bf(f"xt{i % 2}")
                ob, oB = xo[i % 2], bf(f"xo{i % 2}")
                sc.dma("sp", xb[:, :], xv[i], reads=[bf(f"x{l}_{i}")] if l else [], writes=[xB])
                pt, pB = psum()
                ptb = pt[:, :].bitcast(BF16)
                for et in range(8):
                    sc.op("pe", lambda h, ptb=ptb, i=i, et=et: h.transpose(
                        out=ptb[:, et * 128:(et + 1) * 128], in_=GM[:, i, et * 128:(et + 1) * 128],
                        identity=ident), reads=[bf(f"GM{i}"), bf("cb")], writes=[pB], inc=(et == 7))
                sc.op("act", lambda h, ptb=ptb: h.activation(
                    out=hT[:, :, 0:128], in_=ptb.rearrange("p (k t) -> p k t", k=8), func=AF.Copy),
                    reads=[pB], writes=[bf("hT")])
                for k in range(2):
                    py, pyB = psum()
                    for et in range(8):
                        sc.op("pe", lambda h, py=py, et=et, k=k: h.matmul(
                            py[:, :], lhsT=hT[:, et, 0:128], rhs=wo[k][:, et, :],
                            start=(et == 0), stop=(et == 7)),
                            reads=[bf("hT"), bf(f"big{k}")], writes=[pyB], inc=(et == 7))
                    sc.op("dve", lambda h, py=py, ob=ob, xb=xb, k=k: h.tensor_tensor(
                        out=ob[:, k * 512:(k + 1) * 512], in0=py[:, :], in1=xb[:, k * 512:(k + 1) * 512],
                        op=ALU.add), reads=[pyB, xB], writes=[oB])
                toks.append(sc.dma("sp", ov[i], ob[:, :], reads=[oB], writes=[bf(f"x{l + 1}_{i}")]))
            return toks

        convert_weights(0)
        toks = []
        for l in range(depth):
            layer_params(l)
            x_src = x_in if l == 0 else xs_d[(l - 1) % 2]
            last = (l == depth - 1)
            x_dst = out_d if last else xs_d[l % 2]
            ph = DEBUG.get("phases", "afsd")
            if "a" in ph:
                phase_a(l, x_src)
            if l + 1 < depth:
                convert_weights(l + 1)
            if "f" in ph:
                phase_c_fox(l)
            if "s" in ph:
                phase_c_sb(l)
            if "d" in ph:
                toks = phase_d(l, x_src, x_dst, last)
        if dump:
            toks.append(sc.dma("sp", dbg["qt"][:, :, :], qt_d[:, :, :],
                               reads=[bf(f"qt{j}") for j in range(NBLK)]))
            toks.append(sc.dma("sp", dbg["kt"][:, :, :], kt_d[:, :, :],
                               reads=[bf(f"kt{j}") for j in range(NBLK)]))
            toks.append(sc.dma("sp", dbg["gm"].rearrange("(i p) d -> p i d", p=128), GM[:, :, :],
                               reads=[bf(f"GM{i}") for i in range(NT)]))
            toks.append(sc.dma("sp", dbg["c"], c_sb[:, :, :].rearrange("p i h -> p (i h)"),
                               reads=[bf("c_sb")]))
        sc.finish("sp", toks)

        with nc.Block() as block:
            sc.replay(block)
    return nc


_CACHE = {}


def kernel(x, norm_g, w_in, b_f, q_norm_g, k_norm_g, w_pool, pool_scale, w_out):
    depth = DEBUG.get("depth", DEPTH)
    dump = DEBUG.get("dump", False)
    x = np.ascontiguousarray(np.asarray(x, np.float32))
    f32 = lambda a: np.ascontiguousarray(np.asarray(a, np.float32))
    cbh, cfh = _host_consts()
    gcol = f32(np.asarray(norm_g, np.float32).reshape(DEPTH, 8, 128).transpose(2, 0, 1).reshape(128, DEPTH * 8))
    gq_bc = f32(np.broadcast_to(np.asarray(q_norm_g, np.float32).reshape(1, DEPTH * 64), (128, DEPTH * 64)))
    gk_bc = f32(np.broadcast_to(np.asarray(k_norm_g, np.float32).reshape(1, DEPTH * 64), (128, DEPTH * 64)))
    bf_bc = f32(np.broadcast_to(np.asarray(b_f, np.float32).reshape(1, DEPTH * 8), (128, DEPTH * 8)))
    ps_bc = f32(np.broadcast_to(np.asarray(pool_scale, np.float32).reshape(1, DEPTH * 256), (128, DEPTH * 256)))
    key = (depth, dump)
    if key not in _CACHE:
        _CACHE[key] = build_program(depth, dump)
    nc = _CACHE[key]
    shared = dict(w_in=f32(w_in), w_out=f32(w_out), w_pool=f32(w_pool), gcol=gcol, gq_bc=gq_bc,
                  gk_bc=gk_bc, bf_bc=bf_bc, ps_bc=ps_bc, cb=cbh, cf=cfh)
    in_maps = []
    for c in range(8):
        m = dict(shared)
        m["x"] = x[c % 4]
        in_maps.append(m)
    if DEBUG.get("fake_inputs"):
        for m in in_maps:
            for k in ("x", "w_in", "w_out"):
                m.pop(k)
    res = run_bass_kernel_spmd(nc, in_maps, core_ids=list(range(8)))
    out = np.stack([np.asarray(res.results[b]["out"], np.float32) for b in range(4)], axis=0)
    if dump:
        DEBUG["res"] = res.results
    return out
```
